# Optimizing a Trainium2 kernel written in Bass

```python
import jax, jax.numpy as jnp
from jax import lax
import numpy as np

D_MODEL = 2048
BATCH = 16
SEQ = 2048
DEPTH = 1
DEC_BATCH = 32
DEC_SEQ = 64
PAST_LEN = 2048

CHUNK = 64
MIX_WIDTH = D_MODEL
D_CONV = MIX_WIDTH // 2
CONV_GROUPS = 16
CONV_WIDTH = 3
D_ATTN = MIX_WIDTH - D_CONV
HEAD_DIM = 64
N_HEADS = D_ATTN // HEAD_DIM
N_KV_HEADS = 2
GQA_GROUP = N_HEADS // N_KV_HEADS
D_KV = N_KV_HEADS * HEAD_DIM
WINDOW = 128
WINDOW_CHUNKS = WINDOW // CHUNK
ROPE_THETA = 10000.0
D_FF = 5632
LN_EPS = 1e-5
RMS_EPS = 1e-6
ALPHA = (2.0 * DEPTH) ** 0.25
BETA = (8.0 * DEPTH) ** -0.25
ATTN_SCALE = HEAD_DIM ** -0.5
D_IN_MIX = 3 * D_CONV + D_ATTN + 2 * D_KV
SPLITS = [D_CONV, 2 * D_CONV, 3 * D_CONV, 3 * D_CONV + D_ATTN, 3 * D_CONV + D_ATTN + D_KV]

kernel_name = "hybrid_conv_swa_sink_streaming_step"


def _layer_norm(x, g, b):
    xf = x.astype(jnp.float32)
    mu = jnp.mean(xf, axis=-1, keepdims=True)
    var = jnp.mean(jnp.square(xf - mu), axis=-1, keepdims=True)
    return ((xf - mu) * lax.rsqrt(var + LN_EPS) * g.astype(jnp.float32) + b.astype(jnp.float32)).astype(x.dtype)


def _rms_norm(x, g):
    xf = x.astype(jnp.float32)
    inv = lax.rsqrt(jnp.mean(xf * xf, axis=-1, keepdims=True) + RMS_EPS)
    return (xf * inv * g.astype(jnp.float32)).astype(x.dtype)


def _half_ffn_block(x, w_in, w_out, g, b):
    gate, up = jnp.split(x @ w_in, 2, axis=-1)
    return _layer_norm(ALPHA * x + 0.5 * ((jax.nn.silu(gate) * up) @ w_out), g, b)


def _rope(x, pos):
    half = HEAD_DIM // 2
    inv = ROPE_THETA ** (-jnp.arange(half, dtype=jnp.float32) / half)
    ang = pos.astype(jnp.float32)[:, None] * inv[None, :]
    cos = jnp.cos(ang)[None, :, None, :]
    sin = jnp.sin(ang)[None, :, None, :]
    xf = x.astype(jnp.float32)
    x1, x2 = xf[..., :half], xf[..., half:]
    return jnp.concatenate([x1 * cos - x2 * sin, x2 * cos + x1 * sin], axis=-1).astype(x.dtype)


def _mix_project(x, w_mix_in, pos):
    B, T = x.shape[0], x.shape[1]
    b_gate, c_gate, hc, q, k, v = jnp.split(x @ w_mix_in, SPLITS, axis=-1)
    q = _rope(q.reshape(B, T, N_HEADS, HEAD_DIM), pos)
    k = _rope(k.reshape(B, T, N_KV_HEADS, HEAD_DIM), pos)
    v = v.reshape(B, T, N_KV_HEADS, HEAD_DIM)
    return b_gate, c_gate, hc, q, k, v


def _short_conv(b_gate, c_gate, hc, prev, conv_w):
    u = c_gate * hc
    T = u.shape[1]
    u_pad = jnp.concatenate([prev.astype(u.dtype), u], axis=1)
    z = conv_w[0] * u_pad[:, 0:T]
    for j in range(1, CONV_WIDTH):
        z = z + conv_w[j] * u_pad[:, j:j + T]
    return b_gate * z, u_pad[:, -(CONV_WIDTH - 1):]


def _sink_softmax(s, sink):
    m = jnp.maximum(jnp.max(s, axis=-1, keepdims=True), sink)
    e = jnp.exp(s - m)
    return e / (jnp.sum(e, axis=-1, keepdims=True) + jnp.exp(sink - m))


def _window_attn_prompt(q, k, v, sinks):
    B, S = q.shape[0], q.shape[1]
    nc = S // CHUNK
    band = (WINDOW_CHUNKS + 1) * CHUNK
    qb = q.reshape(B, nc, CHUNK, N_KV_HEADS, GQA_GROUP, HEAD_DIM)

    def bands(t):
        tp = jnp.pad(t, ((0, 0), (WINDOW, 0), (0, 0), (0, 0)))
        tc = tp.reshape(B, nc + WINDOW_CHUNKS, CHUNK, N_KV_HEADS, HEAD_DIM)
        return jnp.concatenate([tc[:, j:j + nc] for j in range(WINDOW_CHUNKS + 1)], axis=2)

    kb, vb = bands(k), bands(v)
    s = jnp.einsum('bnqhgd,bnkhd->bnhgqk', qb, kb, preferred_element_type=jnp.float32) * ATTN_SCALE
    key_pos = (jnp.arange(nc)[:, None] - WINDOW_CHUNKS) * CHUNK + jnp.arange(band)[None, :]
    valid = (key_pos >= 0)[None, :, None, None, None, :]
    s = jnp.where(valid, s, jnp.float32(-1e30))
    sink = sinks.astype(jnp.float32).reshape(1, 1, N_KV_HEADS, GQA_GROUP, 1, 1)
    p = _sink_softmax(s, sink).astype(v.dtype)
    o = jnp.einsum('bnhgqk,bnkhd->bnqhgd', p, vb)
    return o.reshape(B, S, D_ATTN)


def _window_attn_sample(q, k_all, v_all, sinks):
    B, T = q.shape[0], q.shape[1]
    qg = q.reshape(B, T, N_KV_HEADS, GQA_GROUP, HEAD_DIM)
    s = jnp.einsum('bqhgd,bkhd->bhgqk', qg, k_all, preferred_element_type=jnp.float32) * ATTN_SCALE
    sink = sinks.astype(jnp.float32).reshape(1, N_KV_HEADS, GQA_GROUP, 1, 1)
    p = _sink_softmax(s, sink).astype(v_all.dtype)
    o = jnp.einsum('bhgqk,bkhd->bqhgd', p, v_all)
    return o.reshape(B, T, D_ATTN)


def _mix_output(y_conv, y_attn, mix_norm_g, w_mix_out):
    y = jnp.concatenate([_rms_norm(y_conv, mix_norm_g[:D_CONV]), _rms_norm(y_attn, mix_norm_g[D_CONV:])], axis=-1)
    return y @ w_mix_out


def setup_inputs(seed: int = 0) -> dict:
    key = jax.random.key(seed)
    ks = jax.random.split(key, 24)
    f32 = jnp.float32
    nrm = lambda k, shape, scale: jax.random.normal(k, shape, f32) * scale
    return {
        "x_prompt": nrm(ks[0], (BATCH, SEQ, D_MODEL), 1.0),
        "x_sample": nrm(ks[1], (DEC_BATCH, DEC_SEQ, D_MODEL), 1.0),
        "cache_conv": nrm(ks[2], (DEPTH, DEC_BATCH, CONV_WIDTH - 1, D_CONV), 1.0),
        "cache_k": nrm(ks[3], (DEPTH, DEC_BATCH, WINDOW, N_KV_HEADS, HEAD_DIM), 1.0),
        "cache_v": nrm(ks[4], (DEPTH, DEC_BATCH, WINDOW, N_KV_HEADS, HEAD_DIM), 1.0),
        "ffn1_w_in": nrm(ks[5], (DEPTH, D_MODEL, 2 * D_FF), D_MODEL ** -0.5),
        "ffn1_w_out": nrm(ks[6], (DEPTH, D_FF, D_MODEL), BETA * D_FF ** -0.5),
        "ln1_g": 1.0 + nrm(ks[7], (DEPTH, D_MODEL), 0.05),
        "ln1_b": nrm(ks[8], (DEPTH, D_MODEL), 0.02),
        "w_mix_in": nrm(ks[9], (DEPTH, D_MODEL, D_IN_MIX), D_MODEL ** -0.5),
        "conv_w": nrm(ks[10], (DEPTH, CONV_WIDTH, D_CONV), CONV_WIDTH ** -0.5),
        "attn_sinks": nrm(ks[11], (DEPTH, N_HEADS), 1.0),
        "mix_norm_g": 1.0 + nrm(ks[12], (DEPTH, MIX_WIDTH), 0.05),
        "w_mix_out": nrm(ks[13], (DEPTH, MIX_WIDTH, D_MODEL), BETA * MIX_WIDTH ** -0.5),
        "ln2_g": 1.0 + nrm(ks[14], (DEPTH, D_MODEL), 0.05),
        "ln2_b": nrm(ks[15], (DEPTH, D_MODEL), 0.02),
        "ffn2_w_in": nrm(ks[16], (DEPTH, D_MODEL, 2 * D_FF), D_MODEL ** -0.5),
        "ffn2_w_out": nrm(ks[17], (DEPTH, D_FF, D_MODEL), BETA * D_FF ** -0.5),
        "ln3_g": 1.0 + nrm(ks[18], (DEPTH, D_MODEL), 0.05),
        "ln3_b": nrm(ks[19], (DEPTH, D_MODEL), 0.02),
    }


def reference(x_prompt, x_sample, cache_conv, cache_k, cache_v,
              ffn1_w_in, ffn1_w_out, ln1_g, ln1_b,
              w_mix_in, conv_w, attn_sinks, mix_norm_g, w_mix_out, ln2_g, ln2_b,
              ffn2_w_in, ffn2_w_out, ln3_g, ln3_b):
    xp, xs = x_prompt, x_sample
    Bp, Sp = xp.shape[0], xp.shape[1]
    Ts = xs.shape[1]
    pos_p = jnp.arange(Sp)
    pos_s = PAST_LEN + jnp.arange(Ts)
    conv_p, k_p, v_p, conv_s, k_s, v_s = [], [], [], [], [], []
    for l in range(DEPTH):
        xp = _half_ffn_block(xp, ffn1_w_in[l], ffn1_w_out[l], ln1_g[l], ln1_b[l])
        xs = _half_ffn_block(xs, ffn1_w_in[l], ffn1_w_out[l], ln1_g[l], ln1_b[l])

        bg, cg, hc, q, k, v = _mix_project(xp, w_mix_in[l], pos_p)
        zeros = jnp.zeros((Bp, CONV_WIDTH - 1, D_CONV), xp.dtype)
        yc, cst = _short_conv(bg, cg, hc, zeros, conv_w[l])
        ya = _window_attn_prompt(q, k, v, attn_sinks[l])
        xp = _layer_norm(ALPHA * xp + _mix_output(yc, ya, mix_norm_g[l], w_mix_out[l]), ln2_g[l], ln2_b[l])
        conv_p.append(cst)
        k_p.append(k[:, -WINDOW:])
        v_p.append(v[:, -WINDOW:])

        bg, cg, hc, q, k, v = _mix_project(xs, w_mix_in[l], pos_s)
        yc, cst = _short_conv(bg, cg, hc, cache_conv[l], conv_w[l])
        k_all = jnp.concatenate([cache_k[l].astype(k.dtype), k], axis=1)
        v_all = jnp.concatenate([cache_v[l].astype(v.dtype), v], axis=1)
        ya = _window_attn_sample(q, k_all, v_all, attn_sinks[l])
        xs = _layer_norm(ALPHA * xs + _mix_output(yc, ya, mix_norm_g[l], w_mix_out[l]), ln2_g[l], ln2_b[l])
        conv_s.append(cst)
        k_s.append(k_all[:, -WINDOW:])
        v_s.append(v_all[:, -WINDOW:])

        xp = _half_ffn_block(xp, ffn2_w_in[l], ffn2_w_out[l], ln3_g[l], ln3_b[l])
        xs = _half_ffn_block(xs, ffn2_w_in[l], ffn2_w_out[l], ln3_g[l], ln3_b[l])
    return (xp, xs, jnp.stack(conv_p), jnp.stack(k_p), jnp.stack(v_p), jnp.stack(conv_s), jnp.stack(k_s), jnp.stack(v_s))
```

```python
import numpy as np
import concourse.bass as bass
import concourse.mybir as mybir
from concourse.bass_utils import run_bass_kernel_spmd

F32 = mybir.dt.float32
BF16 = mybir.dt.bfloat16
AF = mybir.ActivationFunctionType
ALU = mybir.AluOpType

NCORES = 8
D = 2048
DFF = 5632
NKC = 16
NFC = 44
SEQ = 2048
ALPHA = 2.0 ** 0.25
C_FFN = 0.5 / ALPHA
C_MIX = 1.0 / ALPHA
LN_EPS_S = 1e-5 / (ALPHA * ALPHA)
RMS_EPS = 1e-6
NPOS = 2112


class SemC:
    __slots__ = ("h", "count")

    def __init__(self, h):
        self.h = h
        self.count = 0


class Res:
    __slots__ = ("name", "w", "r")

    def __init__(self, name):
        self.name = name
        self.w = None
        self.r = {}


class Eng:
    def __init__(self, nc, name):
        self.name = name
        self.q = []
        self.sem = SemC(nc.alloc_semaphore("prog_" + name))
        self.waited = {}
        self.dsems = []
        self.dnext = 0


class Prog:
    def __init__(self, nc):
        self.nc = nc
        self.pe = Eng(nc, "pe")
        self.act = Eng(nc, "act")
        self.dve = Eng(nc, "dve")
        self.pool = Eng(nc, "pool")
        self.sp = Eng(nc, "sp")
        self.sp.dsems = [SemC(nc.alloc_semaphore(f"dsp{i}")) for i in range(16)]
        self.pool.dsems = [SemC(nc.alloc_semaphore(f"dpl{i}")) for i in range(8)]

    def _waits(self, eng, reads, writes):
        deps = {}
        for r in reads:
            if r.w is not None:
                s, v = r.w
                if deps.get(s, 0) < v:
                    deps[s] = v
        for w in writes:
            if w.w is not None:
                s, v = w.w
                if deps.get(s, 0) < v:
                    deps[s] = v
            for s, v in w.r.items():
                if deps.get(s, 0) < v:
                    deps[s] = v
        for s, v in deps.items():
            self._wait(eng, s, v)

    def _wait(self, eng, s, v):
        if eng.waited.get(s, 0) < v:
            eng.waited[s] = v
            eng.q.append(lambda e, h=s.h, v=v: e.wait_ge(h, v))

    def _mark(self, tok, reads, writes):
        s, v = tok
        for r in reads:
            if r.r.get(s, 0) < v:
                r.r[s] = v
        for w in writes:
            w.w = tok
            w.r = {}

    def op(self, eng, fn, reads=(), writes=()):
        self._waits(eng, reads, writes)
        eng.sem.count += 1
        tok = (eng.sem, eng.sem.count)
        eng.q.append(lambda e, fn=fn, h=eng.sem.h: fn(e).then_inc(h, 1))
        self._mark(tok, reads, writes)

    def group(self, fns, reads=(), writes=()):
        eng = self.pe
        self._waits(eng, reads, writes)
        for fn in fns[:-1]:
            eng.q.append(lambda e, fn=fn: fn(e))
        eng.sem.count += 1
        tok = (eng.sem, eng.sem.count)
        eng.q.append(lambda e, fn=fns[-1], h=eng.sem.h: fn(e).then_inc(h, 1))
        self._mark(tok, reads, writes)

    def dma(self, eng, out, in_, reads=(), writes=(), **kw):
        self._waits(eng, reads, writes)
        s = eng.dsems[eng.dnext % len(eng.dsems)]
        eng.dnext += 1
        if s.count:
            self._wait(eng, s, s.count)
        s.count += 16
        tok = (s, s.count)
        eng.q.append(lambda e, o=out, i=in_, h=s.h, kw=kw: e.dma_start(out=o, in_=i, **kw).then_inc(h, 16))
        self._mark(tok, reads, writes)

    def finish(self):
        for eng in (self.sp, self.pool):
            for s in eng.dsems:
                if s.count:
                    self._wait(self.sp, s, s.count)
        for eng in (self.pe, self.act, self.dve, self.pool):
            if eng.sem.count:
                self._wait(self.sp, eng.sem, eng.sem.count)
        nc = self.nc
        with nc.Block() as block:
            @block.sync
            def _(e):
                for f in self.sp.q:
                    f(e)

            @block.tensor
            def _(e):
                for f in self.pe.q:
                    f(e)

            @block.scalar
            def _(e):
                for f in self.act.q:
                    f(e)

            @block.vector
            def _(e):
                for f in self.dve.q:
                    f(e)

            @block.gpsimd
            def _(e):
                for f in self.pool.q:
                    f(e)


def MM(out, lhsT, rhs, start, stop):
    return lambda e: e.matmul(out, lhsT=lhsT, rhs=rhs, start=start, stop=stop)


def TR(out, in_, ident):
    return lambda e: e.transpose(out=out, in_=in_, identity=ident)


def ACTV(out, in_, func, **kw):
    return lambda e: e.activation(out=out, in_=in_, func=func, **kw)


def TT(out, in0, in1, op):
    return lambda e: e.tensor_tensor(out=out, in0=in0, in1=in1, op=op)


def STT(out, in0, scalar, in1, op0, op1):
    return lambda e: e.scalar_tensor_tensor(out=out, in0=in0, scalar=scalar, in1=in1, op0=op0, op1=op1)


def TS(out, in0, s1, s2, op0, op1=None):
    if op1 is None:
        return lambda e: e.tensor_scalar(out=out, in0=in0, scalar1=s1, scalar2=s2, op0=op0)
    return lambda e: e.tensor_scalar(out=out, in0=in0, scalar1=s1, scalar2=s2, op0=op0, op1=op1)


def CP(out, in_):
    return lambda e: e.tensor_copy(out=out, in_=in_)


def MS(ap, val):
    return lambda e: e.memset(ap, val)


def RCP(out, in_):
    return lambda e: e.reciprocal(out=out, in_=in_)


def BNS(out, in_):
    return lambda e: e.bn_stats(out=out, in_=in_)


def BNA(out, in_):
    return lambda e: e.bn_aggr(out=out, in_=in_)


class _Stop(Exception):
    pass


def build(groups, stop=None):
    def chk(name):
        if stop is not None and name == stop:
            raise _Stop()
    nc = bass.Bass("TRN2", target_bir_lowering=False)
    P = Prog(nc)
    pe, act, dve, pool, sp = P.pe, P.act, P.dve, P.pool, P.sp
    dq = pool

    def din(name, shape, dt=F32):
        return nc.dram_tensor(name, list(shape), dt, kind="ExternalInput").ap()

    def dout(name, shape):
        return nc.dram_tensor(name, list(shape), F32, kind="ExternalOutput").ap()

    xp = din("xp", [2 * SEQ, D])
    xs = din("xs", [256, D])
    cconv = din("cconv", [4, 2, 1024])
    ck = din("ck", [4, 128, 128])
    cv = din("cv", [4, 128, 128])
    W = {
        "w1i": din("w1i", [D, 2 * DFF]), "w1o": din("w1o", [DFF, D]),
        "wmi": din("wmi", [D, 4352]), "wmo": din("wmo", [D, D]),
        "w2i": din("w2i", [D, 2 * DFF]), "w2o": din("w2o", [DFF, D]),
    }
    lnp = {k: din(k, [1, D]) for k in ("ln1g", "ln1b", "ln2g", "ln2b", "ln3g", "ln3b")}
    lnT_d = din("lnT", [128, 96])
    convw_d = din("convw", [128, 24])
    gmix_d = din("gmix", [128, 16])
    sinks_d = din("sinks", [1, 16])
    ropec = din("ropec", [128, NPOS])
    ropes = din("ropes", [128, NPOS])

    yp = dout("yp", [2 * SEQ, D])
    ys = dout("ys", [256, D])
    ncp = dout("ncp", [2, 2, 1024])
    nkp = dout("nkp", [2, 128, 128])
    nvp = dout("nvp", [2, 128, 128])
    ncs = dout("ncs", [4, 2, 1024])
    nks = dout("nks", [4, 128, 128])
    nvs = dout("nvs", [4, 128, 128])

    NT_W = {"w1i": 22, "w1o": 12, "wmi": 11, "wmo": 4, "w2i": 22, "w2o": 12}
    SC = {k: nc.dram_tensor("sc_" + k, [n, 128, 8192], BF16).ap() for k, n in NT_W.items()}
    R_sc = {}

    cur = [16640]

    def sb(name, shape, dt):
        nbytes = int(np.prod(shape[1:])) * (4 if dt == F32 else 2)
        off = cur[0]
        cur[0] = (off + nbytes + 63) // 64 * 64
        assert cur[0] <= 229376, (name, cur[0])
        return nc.alloc_sbuf_tensor_at(name, list(shape), dt, offset=off)

    def sb_at(name, shape, dt, off):
        return nc.alloc_sbuf_tensor_at(name, list(shape), dt, offset=off)

    xres = sb("xres", [128, 4, D], F32)
    xT = sb("xT", [128, NKC, 512], BF16)
    wsl = [sb(f"wsl{i}", [128, 8192], BF16) for i in range(3)]
    offD = cur[0]
    cur[0] += 49152
    hT = sb_at("hT", [128, NFC, 512], BF16, offD)
    sgt = [sb_at(f"sg{i}", [128, 512], F32, offD + 45056 + 2048 * i) for i in range(2)]
    ycT = sb_at("ycT", [128, 8, 512], F32, offD)
    yaT = sb_at("yaT", [128, 8, 512], F32, offD + 16384)
    ycb = sb_at("ycb", [128, 8, 512], BF16, offD + 32768)
    yab = sb_at("yab", [128, 8, 512], BF16, offD + 40960)
    offE = cur[0]
    cur[0] += 16384
    gbc = sb_at("gbc", [128, D], F32, offE)
    bbc = sb_at("bbc", [128, D], F32, offE + 8192)
    PT = [sb_at(f"PT{i}", [128, 16, 256], BF16, offE + 8192 * i) for i in range(2)]
    qTb = sb("qTb", [128, 8, 512], BF16)
    qs = sb("qs", [128, 512], F32)
    qr = sb("qr", [128, 512], F32)
    t1 = sb("t1", [128, 512], F32)
    cosg = sb("cosg", [128, 512], F32)
    sing = sb("sing", [128, 512], F32)
    kdup = [sb(f"kdup{g}", [128, 640], BF16) for g in range(2)]
    vaug = [[[sb(f"va{b}_{g}_{p}", [128, 128], BF16) for p in range(2)] for g in range(2)] for b in range(5)]
    uT = sb("uT", [128, 528], F32)
    cgs = sb("cgs", [128, 512], F32)
    zc = sb("zc", [128, 512], F32)
    sqt = [qs, qr]
    rinv = t1
    rd = sb("rd", [128, 512], F32)
    rd2 = sb("rd2", [128, 512], F32)
    kout = cgs
    vout = zc
    cstage = sb("cstage", [128, 128], F32)
    ident = sb("ident", [128, 128], F32)
    ones_f = sb("ones_f", [128, 128], F32)
    stats = sb("stats", [128, 4, 4, 6], F32)
    mv = sb("mv", [128, 4, 2], F32)
    rs = sb("rs", [128, 4], F32)
    convw = sb("convw_s", [128, 24], F32)
    gmix = sb("gmix_s", [128, 16], F32)
    ucar = sb("ucar", [128, 8, 4, 2], F32)
    sinkL = [sb(f"sinkL{p}", [2, 128], BF16) for p in range(2)]
    esrow = sb("esrow", [2, 16], BF16)
    lnT = sb("lnT_s", [128, 6, 16], F32)
    nmr = sb("nmr", [128, 4], F32)
    xstage = sb("xstage", [128, D], F32)
    sk = sb("sk", [2, 16], F32)
    skb = sb("skb", [2, 16], BF16)
    sk2 = sb("sk2", [2, 16], F32)
    msk = sb("msk", [2, 1], F32)
    Rperm = cstage

    banks = [nc.alloc_psum_tensor(f"bank{i}", [128, 512], F32) for i in range(8)]
    R_bank = [Res(f"bank{i}") for i in range(8)]
    ring = [0]

    reserved = set()

    ring_n = [7]

    def balloc():
        while True:
            b = ring[0] % ring_n[0]
            ring[0] += 1
            if b not in reserved:
                return b

    SSB = 7

    R_x = [Res(f"x{t}") for t in range(4)]
    R_xT = [Res(f"xT{c}") for c in range(NKC)]
    R_h = [Res(f"h{j}") for j in range(NFC)]
    R_w = [Res(f"w{i}") for i in range(3)]
    R_sg = [Res("sg0"), Res("sg1")]
    R_E = [Res("E0"), Res("E1")]
    R_stats = [Res(f"st{t}") for t in range(4)]
    R_mv = Res("mv")
    R_rs = Res("rs")
    R_qs, R_qr, R_t1, R_cs = Res("qs"), Res("qr"), Res("t1"), Res("cs")
    R_qT = [Res(f"qT{i}") for i in range(8)]
    R_kd = [Res("kd0"), Res("kd1")]
    R_va = [Res(f"va{b}") for b in range(5)]
    R_u, R_cg, R_z = Res("u"), Res("cg"), Res("z")
    R_yc = [Res(f"yc{i}") for i in range(8)]
    R_ya = [Res(f"ya{i}") for i in range(8)]
    R_ycb = [Res(f"ycb{i}") for i in range(8)]
    R_yab = [Res(f"yab{i}") for i in range(8)]
    R_sq = [R_qs, R_qr]
    R_rinv, R_rd = R_t1, Res("rd")
    R_rd2 = Res("rd2")
    rdbuf = [(rd, R_rd), (rd2, R_rd2)]
    rdi = [0]
    R_kout, R_vout, R_cst = R_cg, R_z, Res("cst")
    R_const = Res("const")
    R_ucar = Res("ucar")
    R_stage = Res("stage")
    R_nmr = Res("nmr")

    P.op(pool, MS(ident[:], 0.0), writes=[R_const])
    P.op(pool, lambda e: e.affine_select(out=ident[:], in_=ident[:], pattern=[[-1, 128]],
                                         compare_op=ALU.not_equal, fill=1.0, base=0,
                                         channel_multiplier=1), reads=[R_const], writes=[R_const])
    P.op(dve, MS(ones_f[:], 1.0), writes=[R_const])
    for b in range(5):
        for g in range(2):
            for p in range(2):
                P.op(pool, MS(vaug[b][g][p][:], 1.0), writes=[R_va[b]])
    P.op(dve, MS(sinkL[0][:], 1.0), writes=[R_const])
    P.op(dve, MS(sinkL[0][:, 0:64], 0.0), writes=[R_const])
    P.op(dve, MS(sinkL[1][:], 1.0), writes=[R_const])
    P.op(dve, MS(sinkL[1][:, 64:128], 0.0), writes=[R_const])
    P.op(dve, MS(msk[:], 0.0), writes=[R_const])
    P.op(dve, MS(msk[0:1, :], 1.0), writes=[R_const])
    P.dma(sp, convw[:], convw_d, writes=[R_const])
    P.dma(sp, gmix[:], gmix_d, writes=[R_const])
    P.dma(sp, sk[:], sinks_d.to_broadcast([2, 16]), writes=[R_const])
    P.op(act, ACTV(sk[:], sk[:], AF.Exp), reads=[R_const], writes=[R_const])
    P.op(dve, CP(skb[:], sk[:]), reads=[R_const], writes=[R_const])
    P.op(dve, CP(sk2[:], skb[:]), reads=[R_const], writes=[R_const])
    P.op(dve, TT(sk[:], sk[:], sk2[:], ALU.subtract), reads=[R_const], writes=[R_const])
    P.op(dve, TT(sk2[:], sk2[:], sk[:], ALU.subtract), reads=[R_const], writes=[R_const])
    P.op(dve, STT(sk[:], sk2[:], msk[:, 0:1], sk[:], ALU.mult, ALU.add), reads=[R_const], writes=[R_const])
    P.op(dve, CP(esrow[:], sk[:]), reads=[R_const], writes=[R_const])
    P.dma(sp, lnT[:].rearrange("p a b -> p (a b)"), lnT_d, writes=[R_const])

    def wspec(kind, idx):
        Wd = W[kind]
        kp = "(kc p) n -> p kc n"
        if kind in ("w1i", "w2i"):
            c0 = 256 * idx
            return [(0, (16, 256), Wd[:, c0:c0 + 256].rearrange(kp, p=128)),
                    (4096, (16, 256), Wd[:, DFF + c0:DFF + c0 + 256].rearrange(kp, p=128))]
        if kind in ("w1o", "w2o"):
            nb, jb = divmod(idx, 3)
            j0 = 16 * jb
            nj = 16 if jb < 2 else 12
            return [(0, (nj, 512), Wd[j0 * 128:(j0 + nj) * 128, nb * 512:(nb + 1) * 512]
                     .rearrange("(j p) n -> p j n", p=128))]
        if kind == "wmi":
            if idx == 0:
                return [(0, (16, 256), Wd[:, 4096:4352].rearrange(kp, p=128))]
            if idx in (1, 2):
                c0 = 3072 + 512 * (idx - 1)
                return [(0, (16, 512), Wd[:, c0:c0 + 512].rearrange(kp, p=128))]
            i = idx - 3
            return [(0, (16, 128), Wd[:, 1024 + 128 * i:1024 + 128 * i + 128].rearrange(kp, p=128)),
                    (2048, (16, 128), Wd[:, 2048 + 128 * i:2048 + 128 * i + 128].rearrange(kp, p=128)),
                    (4096, (16, 128), Wd[:, 128 * i:128 * i + 128].rearrange(kp, p=128))]
        if kind == "wmo":
            return [(0, (16, 512), Wd[:, idx * 512:(idx + 1) * 512].rearrange(kp, p=128))]
        raise ValueError(kind)

    wlist = []
    for _g in groups:
        for kind in ("w1i", "w1o", "wmi", "wmo", "w2i", "w2o"):
            for idx in range(NT_W[kind]):
                wlist.append((kind, idx))
    wstate = {"next": 0, "issued": 0}

    def wissue(n):
        kind, idx = wlist[n]
        slot = n % 3
        spec = wspec(kind, idx)
        ntot = max(off + a * b for off, (a, b), _ in spec)
        key = (kind, idx)
        if key not in R_sc:
            R_sc[key] = Res(f"sc_{kind}_{idx}")
            for off, (a, b), src in spec:
                dst = wsl[slot][:, off:off + a * b].rearrange("p (a b) -> p a b", b=b)
                P.dma(pool, dst, src, writes=[R_w[slot]])
            P.dma(sp, SC[kind][idx][:, 0:ntot], wsl[slot][:, 0:ntot], reads=[R_w[slot]], writes=[R_sc[key]])
        else:
            P.dma(sp, wsl[slot][:, 0:ntot], SC[kind][idx][:, 0:ntot], reads=[R_sc[key]], writes=[R_w[slot]])

    def wget(kind, idx):
        n = wstate["next"]
        assert wlist[n] == (kind, idx), (wlist[n], kind, idx)
        wstate["next"] += 1
        while wstate["issued"] < min(len(wlist), n + 3):
            wissue(wstate["issued"])
            wstate["issued"] += 1
        return n % 3

    def wview(slot, off, a, b):
        return wsl[slot][:, off:off + a * b].rearrange("p (a b) -> p a b", b=b)

    evac_rr = [0]

    def evac_copy(out, in_, reads, writes, eng=None):
        evac_rr[0] += 1
        use_act = (evac_rr[0] % 2 == 1) if eng is None else (eng == "act")
        if use_act:
            P.op(act, ACTV(out, in_, AF.Copy), reads=reads, writes=writes)
        else:
            P.op(dve, CP(out, in_), reads=reads, writes=writes)

    def to_xT(NT):
        for t in range(NT):
            xT_tile(t, lambda c, t=t: xres[:, t, c * 128:(c + 1) * 128], [R_x[t]], None)

    def xT_tile(t, src, src_res, ln_idx):
        for q in range(4):
            b = balloc()
            fns = [TR(banks[b][:, cc * 128:(cc + 1) * 128], src(4 * q + cc), ident[:]) for cc in range(4)]
            P.group(fns, reads=list(src_res) + [R_const], writes=[R_bank[b]])
            if ln_idx is None:
                evac_copy(xT[:, 4 * q:4 * q + 4, t * 128:(t + 1) * 128],
                          banks[b][:, :].rearrange("p (a c) -> p a c", c=128), [R_bank[b]],
                          [R_xT[4 * q + cc] for cc in range(4)], eng=("act" if q % 2 == 0 else "dve"))
            else:
                for cc in range(4):
                    c = 4 * q + cc
                    gs = lnT[:, 2 * ln_idx, c:c + 1]
                    bs = lnT[:, 2 * ln_idx + 1, c:c + 1]
                    if q % 2 == 0:
                        P.op(act, ACTV(xT[:, c, t * 128:(t + 1) * 128], banks[b][:, cc * 128:(cc + 1) * 128],
                                       AF.Identity, scale=gs, bias=bs), reads=[R_bank[b], R_const], writes=[R_xT[c]])
                    else:
                        P.op(dve, TS(xT[:, c, t * 128:(t + 1) * 128], banks[b][:, cc * 128:(cc + 1) * 128],
                                     gs, bs, ALU.mult, ALU.add), reads=[R_bank[b], R_const], writes=[R_xT[c]])

    def load_ln(gname, bname):
        P.dma(dq, gbc[:], lnp[gname].to_broadcast([128, D]), writes=[R_E[0]])
        P.dma(dq, bbc[:], lnp[bname].to_broadcast([128, D]), writes=[R_E[1]])

    def ffn_in(kind, NT):
        ntok = NT * 128
        for p in range(22):
            slot = wget(kind, p)
            wg = wview(slot, 0, 16, 256)
            wu = wview(slot, 4096, 16, 256)
            for jj in range(2):
                j = 2 * p + jj
                bg, bu = balloc(), balloc()
                fns = []
                for kc in range(NKC):
                    fns.append(MM(banks[bg][:, 0:ntok], wg[:, kc, jj * 128:(jj + 1) * 128], xT[:, kc, 0:ntok],
                                  kc == 0, kc == NKC - 1))
                for kc in range(NKC):
                    fns.append(MM(banks[bu][:, 0:ntok], wu[:, kc, jj * 128:(jj + 1) * 128], xT[:, kc, 0:ntok],
                                  kc == 0, kc == NKC - 1))
                P.group(fns, reads=[R_w[slot]] + R_xT, writes=[R_bank[bg], R_bank[bu]])
                k = j % 2
                P.op(act, ACTV(sgt[k][:, 0:ntok], banks[bg][:, 0:ntok], AF.Silu), reads=[R_bank[bg]], writes=[R_sg[k]])
                P.op(dve, TT(hT[:, j, 0:ntok], sgt[k][:, 0:ntok], banks[bu][:, 0:ntok], ALU.mult),
                     reads=[R_sg[k], R_bank[bu]], writes=[R_h[j]])

    def out_ln(kind, NT, nk, stat, stat_res, cres, ln_idx, final, ydst, nxt):
        tiles_per_nb = 3 if kind in ("w1o", "w2o") else 1
        for nb in range(4):
            if nxt is not None and nb < nxt["NT"]:
                P.dma(sp, xstage[:], nxt["xsrc"][nb * 128:(nb + 1) * 128, :], writes=[R_stage])
            bk = [balloc() for _ in range(NT)]
            k0 = 0
            for jb in range(tiles_per_nb):
                slot = wget(kind, nb * tiles_per_nb + jb)
                nj = (16 if jb < 2 else 12) if tiles_per_nb == 3 else 16
                wv = wview(slot, 0, nj, 512)
                halves = ((0, nj),) if kind != "wmo" else ((0, 8), (8, 16))
                for (ka, kb_) in halves:
                    for t in range(NT):
                        fns = [MM(banks[bk[t]][:, :], stat(k0 + kk, t), wv[:, kk, :], (k0 + kk == 0), (k0 + kk == nk - 1))
                               for kk in range(ka, kb_)]
                        P.group(fns, reads=[R_w[slot]] + stat_res[k0 + ka:k0 + kb_], writes=[R_bank[bk[t]]])
                k0 += nj
            sl = slice(nb * 512, (nb + 1) * 512)
            for t in range(NT):
                P.op(dve, STT(xres[:, t, sl], banks[bk[t]][:, :], cres, xres[:, t, sl], ALU.mult, ALU.add),
                     reads=[R_bank[bk[t]], R_x[t]], writes=[R_x[t]])
                P.op(dve, BNS(stats[:, t, nb, :], xres[:, t, sl]), reads=[R_x[t]], writes=[R_stats[t]])
            if nxt is not None and nb < nxt["NT"]:
                prefetch_xT(nxt, nb)
        for t in range(NT):
            P.op(dve, BNA(mv[:, t, :], stats[:, t, :, :].rearrange("p a b -> p (a b)")), reads=[R_stats[t]], writes=[R_mv])
        P.op(dve, TS(rs[:, 0:NT], mv[:, 0:NT, 1], LN_EPS_S, None, ALU.add), reads=[R_mv], writes=[R_rs])
        P.op(act, ACTV(rs[:, 0:NT], rs[:, 0:NT], AF.Ln), reads=[R_rs], writes=[R_rs])
        P.op(act, ACTV(rs[:, 0:NT], rs[:, 0:NT], AF.Exp, scale=-0.5), reads=[R_rs], writes=[R_rs])
        P.op(dve, STT(nmr[:, 0:NT], mv[:, 0:NT, 0], -1.0, rs[:, 0:NT], ALU.mult, ALU.mult),
             reads=[R_mv, R_rs], writes=[R_nmr])
        chk("ln_stats")
        for t in range(NT):
            if t % 2 == 0:
                P.op(act, ACTV(xres[:, t, :], xres[:, t, :], AF.Identity, scale=rs[:, t:t + 1], bias=nmr[:, t:t + 1]),
                     reads=[R_x[t], R_rs, R_nmr], writes=[R_x[t]])
            else:
                P.op(dve, TS(xres[:, t, :], xres[:, t, :], rs[:, t:t + 1], nmr[:, t:t + 1], ALU.mult, ALU.add),
                     reads=[R_x[t], R_rs, R_nmr], writes=[R_x[t]])
        chk("ln_norm")
        if not final:
            for t in range(NT):
                xT_tile(t, lambda c, t=t: xres[:, t, c * 128:(c + 1) * 128], [R_x[t]], ln_idx)
        chk("ln_xT")
        for t in range(NT):
            P.op(pool, TT(xres[:, t, :], xres[:, t, :], gbc[:], ALU.mult), reads=[R_x[t], R_E[0]], writes=[R_x[t]])
            P.op(pool, TT(xres[:, t, :], xres[:, t, :], bbc[:], ALU.add), reads=[R_x[t], R_E[1]], writes=[R_x[t]])
        if final:
            for t in range(NT):
                P.dma(dq, ydst[t * 128:(t + 1) * 128, :], xres[:, t, :], reads=[R_x[t]])

    def prefetch_xT(nxt, t):
        xT_tile(t, lambda c: xstage[:, c * 128:(c + 1) * 128], [R_stage], None)

    def rope_a(b, ntok):
        P.op(act, ACTV(qs[:, 0:ntok], banks[b][:, 0:ntok], AF.Copy), reads=[R_bank[b]], writes=[R_qs])

    def rope_b(ntok, out_ap, out_res):
        b2 = balloc()
        P.group([MM(banks[b2][:, 0:ntok], Rperm[:], qs[:, 0:ntok], True, True)], reads=[R_qs, R_cst],
                writes=[R_bank[b2]])
        P.op(dve, TT(t1[:, 0:ntok], qs[:, 0:ntok], cosg[:, 0:ntok], ALU.mult), reads=[R_qs, R_cs], writes=[R_t1])
        P.op(dve, TT(qr[:, 0:ntok], sing[:, 0:ntok], banks[b2][:, 0:ntok], ALU.mult), reads=[R_bank[b2], R_cs],
             writes=[R_qr])
        P.op(dve, TT(out_ap, t1[:, 0:ntok], qr[:, 0:ntok], ALU.add), reads=[R_t1, R_qr], writes=[out_res])

    def rms_scale(NT, yT, R_y, yb, R_yb, goff):
        ntok = NT * 128
        P.op(act, ACTV(rinv[:, 0:ntok], yT[:, 0, 0:ntok], AF.Square), reads=[R_y[0]], writes=[R_rinv])
        for i in range(1, 8):
            k = i % 2
            P.op(act, ACTV(sqt[k][:, 0:ntok], yT[:, i, 0:ntok], AF.Square), reads=[R_y[i]], writes=[R_sq[k]])
            P.op(dve, TT(rinv[:, 0:ntok], rinv[:, 0:ntok], sqt[k][:, 0:ntok], ALU.add), reads=[R_rinv, R_sq[k]],
                 writes=[R_rinv])
        P.group([MM(banks[SSB][:, 0:ntok], ones_f[:], rinv[:, 0:ntok], True, True)],
                reads=[R_rinv, R_const], writes=[R_bank[SSB]])
        P.op(dve, TS(rinv[:, 0:ntok], banks[SSB][:, 0:ntok], 1.0 / 1024.0, RMS_EPS, ALU.mult, ALU.add),
             reads=[R_bank[SSB]], writes=[R_rinv])
        P.op(act, ACTV(rinv[:, 0:ntok], rinv[:, 0:ntok], AF.Ln), reads=[R_rinv], writes=[R_rinv])
        P.op(act, ACTV(rinv[:, 0:ntok], rinv[:, 0:ntok], AF.Exp, scale=-0.5), reads=[R_rinv], writes=[R_rinv])
        for i in range(8):
            P.op(dve, STT(yb[:, i, 0:ntok], yT[:, i, 0:ntok], gmix[:, goff + i:goff + i + 1], rinv[:, 0:ntok],
                          ALU.mult, ALU.mult), reads=[R_y[i], R_rinv, R_const], writes=[R_yb[i]])

    pend_norm = []

    def flush_norms():
        items = list(pend_norm)
        del pend_norm[:]
        for i0 in range(0, len(items), 2):
            pair = items[i0:i0 + 2]
            st = [pv_norm_a(*a_) for a_ in pair]
            for x in st:
                pv_norm_b(x)
            for x in st:
                pv_norm_c(x)
            for a_ in pair:
                reserved.discard(a_[0])

    def defer_norm(*a_):
        pend_norm.append(a_)
        reserved.add(a_[0])

    def pv_norm_a(b, par, g, ncols, qsl, nh_cols):
        orow = slice(0, 64) if par == 0 else slice(64, 128)
        drow = slice(64, 128) if par == 0 else slice(0, 64)
        rb, R_rb = rdbuf[rdi[0] % 2]
        rdi[0] += 1
        P.op(dve, CP(rb[orow, 0:ncols], banks[b][drow, 0:ncols]), reads=[R_bank[b]], writes=[R_rb])
        return (b, g, ncols, qsl, nh_cols, orow, rb, R_rb)

    def pv_norm_b(x):
        b, g, ncols, qsl, nh_cols, orow, rb, R_rb = x
        P.op(act, ACTV(rb[orow, 0:ncols], rb[orow, 0:ncols], AF.Ln), reads=[R_rb], writes=[R_rb])
        P.op(act, ACTV(rb[orow, 0:ncols], rb[orow, 0:ncols], AF.Exp, scale=-1.0), reads=[R_rb], writes=[R_rb])

    def pv_norm_c(x):
        b, g, ncols, qsl, nh_cols, orow, rb, R_rb = x
        P.op(dve, TT(yaT[orow, 4 * g:4 * g + 4, qsl],
                     banks[b][orow, 0:ncols].rearrange("p (a b) -> p a b", b=nh_cols),
                     rb[orow, 0:ncols].rearrange("p (a b) -> p a b", b=nh_cols), ALU.mult),
             reads=[R_bank[b], R_rb], writes=[R_ya[4 * g + jj] for jj in range(4)])

    def load_rope(G):
        for (d0, s0) in ((0, 32), (32, 0), (64, 96), (96, 64)):
            P.op(pool, CP(Rperm[:, d0:d0 + 32], ident[:, s0:s0 + 32]), reads=[R_const], writes=[R_cst])
        if G["kind"] == "p":
            p0 = G["gi"] * 512
            P.dma(dq, cosg[:, 0:512], ropec[:, p0:p0 + 512], writes=[R_cs])
            P.dma(dq, sing[:, 0:512], ropes[:, p0:p0 + 512], writes=[R_cs])
        else:
            for s_ in range(4):
                P.dma(dq, cosg[:, s_ * 64:(s_ + 1) * 64], ropec[:, 2048:2112], writes=[R_cs])
                P.dma(dq, sing[:, s_ * 64:(s_ + 1) * 64], ropes[:, 2048:2112], writes=[R_cs])

    def mix(G):
        kind = G["kind"]
        NT = 4 if kind == "p" else 2
        ntok = NT * 128
        NSEQ = 1 if kind == "p" else 4
        L = 512 if kind == "p" else 64
        gi = G["gi"]
        slot = wget("wmi", 0)
        wkv = wview(slot, 0, 16, 256)
        b = balloc()
        P.group([MM(banks[b][:, 0:ntok], wkv[:, kc, 0:128], xT[:, kc, 0:ntok], kc == 0, kc == NKC - 1)
                 for kc in range(NKC)], reads=[R_w[slot]] + R_xT, writes=[R_bank[b]])
        rope_a(b, ntok)
        rope_b(ntok, rd[:, 0:ntok], R_rd)
        for g in range(2):
            for half in range(2):
                eng = dve if half == 0 else pool
                P.op(eng, CP(kdup[g][half * 64:(half + 1) * 64, 128:128 + ntok], rd[g * 64:(g + 1) * 64, 0:ntok]),
                     reads=[R_rd], writes=[R_kd[g]])
        chk("kv_kd")
        if kind == "p" and G["last"]:
            bt = balloc()
            P.group([TR(banks[bt][:, 0:128], rd[:, 384:512], ident[:])], reads=[R_rd, R_const], writes=[R_bank[bt]])
            P.op(act, ACTV(kout[:, 0:128], banks[bt][:, 0:128], AF.Copy), reads=[R_bank[bt]], writes=[R_kout])
            P.dma(dq, nkp[G["seq"]], kout[:, 0:128], reads=[R_kout])
        if kind == "s":
            bt = balloc()
            P.group([TR(banks[bt][0:64, s * 128:(s + 1) * 128], rd[:, s * 64:(s + 1) * 64], ident[:]) for s in range(4)],
                    reads=[R_rd, R_const], writes=[R_bank[bt]])
            P.op(act, ACTV(kout[0:64, :], banks[bt][0:64, :], AF.Copy), reads=[R_bank[bt]], writes=[R_kout])
            P.dma(dq, nks[:, 64:128, :].rearrange("s t f -> t s f"), kout[0:64, :].rearrange("p (s f) -> p s f", f=128),
                  reads=[R_kout])
            chk("kv_kout")
            stg = uT[0:64, 0:512].rearrange("p (s f) -> p s f", f=128)
            for src_, dst_ in ((ck, nks), (cv, nvs)):
                P.dma(dq, stg, src_[:, 64:128, :].rearrange("s t f -> t s f"), writes=[R_u])
                P.dma(dq, dst_[:, 0:64, :].rearrange("s t f -> t s f"), stg, reads=[R_u])
        chk("kv_roll")
        bv = balloc()
        if kind == "p":
            fns = []
            for t in range(NT):
                for kc in range(NKC):
                    fns.append(MM(banks[bv][:, t * 128:(t + 1) * 128], xT[:, kc, t * 128:(t + 1) * 128],
                                  wkv[:, kc, 128:256], kc == 0, kc == NKC - 1))
            P.group(fns, reads=[R_w[slot]] + R_xT, writes=[R_bank[bv]])
            for t in range(NT):
                for g in range(2):
                    for p in range(2):
                        csl = slice(0, 64) if p == 0 else slice(64, 128)
                        evac_copy(vaug[t + 1][g][p][:, csl], banks[bv][:, t * 128 + g * 64:t * 128 + g * 64 + 64],
                                  [R_bank[bv]], [R_va[t + 1]], eng="act")
            if G["last"]:
                P.op(act, ACTV(vout[:, 0:128], banks[bv][:, 384:512], AF.Copy), reads=[R_bank[bv]], writes=[R_vout])
                P.dma(dq, nvp[G["seq"]], vout[:, 0:128], reads=[R_vout])
        else:
            fns = []
            for s in range(4):
                for kc in range(NKC):
                    fns.append(MM(banks[bv][0:64, s * 128:(s + 1) * 128], xT[:, kc, s * 64:(s + 1) * 64],
                                  wkv[:, kc, 128:256], kc == 0, kc == NKC - 1))
            P.group(fns, reads=[R_w[slot]] + R_xT, writes=[R_bank[bv]])
            chk("kv_vmm")
            for s in range(4):
                for g in range(2):
                    for p in range(2):
                        csl = slice(0, 64) if p == 0 else slice(64, 128)
                        evac_copy(vaug[s][g][p][0:64, csl], banks[bv][0:64, s * 128 + g * 64:s * 128 + g * 64 + 64],
                                  [R_bank[bv]], [R_va[s]], eng="dve")
            chk("kv_vaug")
            P.op(dve, CP(vout[0:64, :], banks[bv][0:64, :]), reads=[R_bank[bv]], writes=[R_vout])
            chk("kv_vact")
            P.dma(dq, nvs[:, 64:128, :].rearrange("s t f -> t s f"), vout[0:64, :].rearrange("p (s f) -> p s f", f=128),
                  reads=[R_vout])

        chk("mix_kv")
        for qt in range(2):
            slot = wget("wmi", 1 + qt)
            wq = wview(slot, 0, 16, 512)
            for ii in range(4):
                i = 4 * qt + ii
                b = balloc()
                P.group([MM(banks[b][:, 0:ntok], wq[:, kc, ii * 128:(ii + 1) * 128], xT[:, kc, 0:ntok],
                            kc == 0, kc == NKC - 1) for kc in range(NKC)],
                        reads=[R_w[slot]] + R_xT, writes=[R_bank[b]])
                if i > 0:
                    rope_b(ntok, qTb[:, i - 1, 0:ntok], R_qT[i - 1])
                rope_a(b, ntok)
        rope_b(ntok, qTb[:, 7, 0:ntok], R_qT[7])

        chk("mix_q")
        uv = uT[:, 0:NSEQ * (L + 2)].rearrange("p (s l) -> p s l", l=L + 2)
        zv = zc[:, 0:ntok].rearrange("p (s l) -> p s l", l=L)
        if kind == "s":
            for s in range(4):
                for r in range(2):
                    P.dma(dq, ucar[:, :, s, r], cconv[s, r].rearrange("(i p) -> p i", p=128), writes=[R_ucar],
                          allow_slow_non_contiguous=True)
        agen = attn_prompt(G) if kind == "p" else attn_sample(G)
        ring_n[0] = 8
        for i in range(8):
            if i > 0:
                next(agen, None)
            if next(agen, "done") == "done":
                flush_norms()
            slot = wget("wmi", 3 + i)
            wc = wview(slot, 0, 16, 128)
            wh = wview(slot, 2048, 16, 128)
            wb = wview(slot, 4096, 16, 128)
            b1, b2, b3 = balloc(), balloc(), balloc()
            fns = []
            for (wv_, bb) in ((wc, b1), (wh, b2), (wb, b3)):
                for kc in range(NKC):
                    fns.append(MM(banks[bb][:, 0:ntok], wv_[:, kc, :], xT[:, kc, 0:ntok], kc == 0, kc == NKC - 1))
            P.group(fns, reads=[R_w[slot]] + R_xT, writes=[R_bank[b1], R_bank[b2], R_bank[b3]])
            P.op(act, ACTV(cgs[:, 0:ntok], banks[b1][:, 0:ntok], AF.Copy), reads=[R_bank[b1]], writes=[R_cg])
            if kind == "p" and gi == 0:
                P.op(pool, MS(uT[:, 0:2], 0.0), writes=[R_u])
            else:
                P.op(pool, CP(uv[:, :, 0:2], ucar[:, i, 0:NSEQ, :]), reads=[R_ucar], writes=[R_u])
            P.op(dve, TT(uv[:, :, 2:L + 2], cgs[:, 0:ntok].rearrange("p (s l) -> p s l", l=L),
                         banks[b2][:, 0:ntok].rearrange("p (s l) -> p s l", l=L), ALU.mult),
                 reads=[R_cg, R_bank[b2]], writes=[R_u])
            P.op(dve, TS(zv, uv[:, :, 0:L], convw[:, 3 * i:3 * i + 1], None, ALU.mult),
                 reads=[R_u, R_const], writes=[R_z])
            P.op(dve, STT(zv, uv[:, :, 1:L + 1], convw[:, 3 * i + 1:3 * i + 2], zv, ALU.mult, ALU.add),
                 reads=[R_u, R_z, R_const], writes=[R_z])
            P.op(dve, STT(zv, uv[:, :, 2:L + 2], convw[:, 3 * i + 2:3 * i + 3], zv, ALU.mult, ALU.add),
                 reads=[R_u, R_z, R_const], writes=[R_z])
            P.op(pool, CP(ucar[:, i, 0:NSEQ, :], uv[:, :, L:L + 2]), reads=[R_u], writes=[R_ucar])
            P.op(dve, TT(ycT[:, i, 0:ntok], zc[:, 0:ntok], banks[b3][:, 0:ntok], ALU.mult),
                 reads=[R_z, R_bank[b3]], writes=[R_yc[i]])
        ring_n[0] = 7
        rms_scale(NT, ycT, R_yc, ycb, R_ycb, 0)
        for _ in agen:
            pass
        flush_norms()
        if kind == "p" and G["last"]:
            for r in range(2):
                P.dma(dq, ncp[G["seq"], r].rearrange("(i p) -> p i", p=128), ucar[:, :, 0, r], reads=[R_ucar],
                      allow_slow_non_contiguous=True)
        if kind == "s":
            for s in range(4):
                for r in range(2):
                    P.dma(dq, ncs[s, r].rearrange("(i p) -> p i", p=128), ucar[:, :, s, r], reads=[R_ucar],
                          allow_slow_non_contiguous=True)

        chk("mix_attn")
        rms_scale(NT, yaT, R_ya, yab, R_yab, 8)
        if kind == "p" and not G["last"]:
            for g in range(2):
                P.op(pool, CP(kdup[g][:, 0:128], kdup[g][:, 512:640]), reads=[R_kd[g]], writes=[R_kd[g]])
                for p in range(2):
                    P.op(pool, CP(vaug[0][g][p][:], vaug[4][g][p][:]), reads=[R_va[4]], writes=[R_va[0]])

    def attn_prompt(G):
        gi = G["gi"]
        kb0 = 0 if gi > 0 else 1
        info = {}
        for kbi in range(kb0, 5):
            kb = kbi - 1
            lo = max(0, 2 * kb)
            hi = min(8, 2 * kb + 4)
            ncols = (hi - lo) * 64
            col0 = lo * 64
            info[kbi] = lo
            pt = PT[kbi % 2]
            RE = R_E[kbi % 2]
            for ip in range(4):
                bA, bB = balloc(), balloc()
                g = ip // 2
                fns = []
                for hh in range(2):
                    hq = 2 * ip + hh
                    for e_, bb in ((0, bA), (1, bB)):
                        fns.append(MM(banks[bb][:, hh * 256:hh * 256 + ncols],
                                      kdup[g][e_ * 64:(e_ + 1) * 64, kbi * 128:(kbi + 1) * 128],
                                      qTb[e_ * 64:(e_ + 1) * 64, hq, col0:col0 + ncols], True, True))
                P.group(fns, reads=[R_kd[g], R_qT[2 * ip], R_qT[2 * ip + 1]], writes=[R_bank[bA], R_bank[bB]])
                for e_, bb in ((0, bA), (1, bB)):
                    P.op(act, ACTV(pt[:, 4 * ip + e_:4 * ip + e_ + 3:2, 0:ncols],
                                   banks[bb][:, :].rearrange("p (a c) -> p a c", c=256)[:, :, 0:ncols],
                                   AF.Exp, scale=0.125), reads=[R_bank[bb]], writes=[RE])
            chk("at_s")
            for half in range(2):
                kc_ = 2 * kb + half
                for qc in range(lo, hi):
                    if not (kc_ <= qc <= kc_ + 2):
                        c0 = (qc - lo) * 64
                        P.op(pool, MS(pt[half * 64:(half + 1) * 64, :, c0:c0 + 64], 0.0), writes=[RE])
            flush_norms()
            yield
            m = kbi - 1
            if m < 0:
                yield
                continue
            qsl = slice(m * 128, (m + 1) * 128)
            contrib = []
            if (kbi - 1) in info:
                contrib.append((kbi - 1, (2 * m - info[kbi - 1]) * 64))
            contrib.append((kbi, (2 * m - lo) * 64))
            for g in range(2):
                for par in range(2):
                    b = balloc()
                    h0 = 8 * g + par
                    ob = banks[b][:, :].rearrange("p (a c) -> p a c", c=128)
                    fns = []
                    for ci, (kk, c0) in enumerate(contrib):
                        fns.append(MM(ob, vaug[kk][g][par][:], PT[kk % 2][:, h0:h0 + 7:2, c0:c0 + 128], ci == 0, False))
                    fns.append(MM(ob, sinkL[par][:], esrow[:, h0:h0 + 7:2].unsqueeze(2).to_broadcast([2, 4, 128]), False, True))
                    P.group(fns, reads=[R_E[kk % 2] for kk, _ in contrib] + [R_va[kk] for kk, _ in contrib] + [R_const],
                            writes=[R_bank[b]])
                    defer_norm(b, par, g, 512, qsl, 128)
            yield

    def attn_sample(G):
        for s in range(4):
            P.dma(dq, cstage[:], ck[s], writes=[R_cst])
            bt = balloc()
            P.group([TR(banks[bt][:, 0:128], cstage[:], ident[:])], reads=[R_cst, R_const], writes=[R_bank[bt]])
            for g in range(2):
                for half in range(2):
                    P.op(dve, CP(kdup[g][half * 64:(half + 1) * 64, 384:512], banks[bt][g * 64:(g + 1) * 64, 0:128]),
                         reads=[R_bank[bt]], writes=[R_kd[g]])
            P.dma(dq, cstage[:], cv[s], writes=[R_cst])
            for g in range(2):
                for p in range(2):
                    csl = slice(0, 64) if p == 0 else slice(64, 128)
                    evac_copy(vaug[4][g][p][:, csl], cstage[:, g * 64:(g + 1) * 64], [R_cst], [R_va[4]])
            for blk in range(2):
                nk_ = 128 if blk == 0 else 64
                ksl = slice(384, 512) if blk == 0 else slice(128 + s * 64, 128 + (s + 1) * 64)
                bA, bB = balloc(), balloc()
                fns = []
                for hh in range(8):
                    for e_, bb in ((0, bA), (1, bB)):
                        h = 2 * hh + e_
                        g = h // 8
                        fns.append(MM(banks[bb][0:nk_, hh * 64:(hh + 1) * 64], kdup[g][e_ * 64:(e_ + 1) * 64, ksl],
                                      qTb[e_ * 64:(e_ + 1) * 64, hh, s * 64:(s + 1) * 64], True, True))
                P.group(fns, reads=R_kd + R_qT, writes=[R_bank[bA], R_bank[bB]])
                for e_, bb in ((0, bA), (1, bB)):
                    P.op(act, ACTV(PT[blk][0:nk_, e_:16:2, 0:64],
                                   banks[bb][0:nk_, :].rearrange("p (a c) -> p a c", c=64), AF.Exp, scale=0.125),
                         reads=[R_bank[bb]], writes=[R_E[blk]])
            flush_norms()
            yield
            qsl = slice(s * 64, (s + 1) * 64)
            for g in range(2):
                for par in range(2):
                    b = balloc()
                    h0 = 8 * g + par
                    ob = banks[b][:, 0:256].rearrange("p (a c) -> p a c", c=64)
                    fns = [MM(ob, vaug[4][g][par][:], PT[0][:, h0:h0 + 7:2, 0:64], True, False),
                           MM(ob, vaug[s][g][par][0:64, :], PT[1][0:64, h0:h0 + 7:2, 0:64], False, False),
                           MM(ob, sinkL[par][:], esrow[:, h0:h0 + 7:2].unsqueeze(2).to_broadcast([2, 4, 64]), False, True)]
                    P.group(fns, reads=[R_E[0], R_E[1], R_va[4], R_va[s], R_const], writes=[R_bank[b]])
                    defer_norm(b, par, g, 256, qsl, 64)
            yield

    RY = R_ycb + R_yab

    def mstat(k, t):
        return ycb[:, k, t * 128:(t + 1) * 128] if k < 8 else yab[:, k - 8, t * 128:(t + 1) * 128]

    def hstat(k, t):
        return hT[:, k, t * 128:(t + 1) * 128]

    for G in groups:
        G["NT"] = 4 if G["kind"] == "p" else 2
        if G["kind"] == "p":
            r0 = G["seq"] * SEQ + G["gi"] * 512
            G["xsrc"], G["ydst"] = xp[r0:r0 + 512, :], yp[r0:r0 + 512, :]
        else:
            G["xsrc"], G["ydst"] = xs, ys
    try:
        for gidx, G in enumerate(groups):
            NT = G["NT"]
            xsrc, ydst = G["xsrc"], G["ydst"]
            nxt = groups[gidx + 1] if gidx + 1 < len(groups) else None
            chk("setup")
            for t in range(NT):
                P.dma(dq, xres[:, t, :], xsrc[t * 128:(t + 1) * 128, :], writes=[R_x[t]])
            chk("xload")
            if gidx == 0:
                to_xT(NT)
            chk("xT")
            load_ln("ln1g", "ln1b")
            load_rope(G)
            ffn_in("w1i", NT)
            chk("ffn1in")
            out_ln("w1o", NT, NFC, hstat, R_h, C_FFN, 0, False, None, None)
            chk("ffn1out")
            mix(G)
            chk("mix")
            load_ln("ln2g", "ln2b")
            out_ln("wmo", NT, 16, mstat, RY, C_MIX, 1, False, None, None)
            chk("mixout")
            load_ln("ln3g", "ln3b")
            ffn_in("w2i", NT)
            chk("ffn2in")
            out_ln("w2o", NT, NFC, hstat, R_h, C_FFN, 2, True, ydst, nxt)
    except _Stop:
        pass
    P.finish()
    return nc


def full_groups():
    gs = []
    for s in range(2):
        for gi in range(4):
            gs.append({"kind": "p", "seq": s, "gi": gi, "last": gi == 3})
    gs.append({"kind": "s", "seq": 0, "gi": 0, "last": True})
    return gs


def rope_tables():
    half = 32
    inv = (np.float32(10000.0) ** (-(np.arange(half, dtype=np.float32) / np.float32(half)))).astype(np.float32)
    pos = np.arange(NPOS, dtype=np.float32)
    ang = (pos[None, :] * inv[:, None]).astype(np.float32)
    cos = np.cos(ang).astype(np.float32)
    sin = np.sin(ang).astype(np.float32)
    c = np.zeros((128, NPOS), np.float32)
    s = np.zeros((128, NPOS), np.float32)
    for p in range(128):
        d = p % 64
        c[p] = cos[d % 32]
        s[p] = -sin[d % 32] if d < 32 else sin[d % 32]
    return c, s


def make_in_maps(inputs, ncores=NCORES):
    f = lambda a: np.ascontiguousarray(np.asarray(a, dtype=np.float32))
    c_t, s_t = rope_tables()
    shared = {
        "w1i": f(inputs["ffn1_w_in"][0]), "w1o": f(inputs["ffn1_w_out"][0]),
        "wmi": f(inputs["w_mix_in"][0]), "wmo": f(inputs["w_mix_out"][0]),
        "w2i": f(inputs["ffn2_w_in"][0]), "w2o": f(inputs["ffn2_w_out"][0]),
        "ln1g": f(inputs["ln1_g"]), "ln1b": f(inputs["ln1_b"]),
        "ln2g": f(inputs["ln2_g"]), "ln2b": f(inputs["ln2_b"]),
        "ln3g": f(inputs["ln3_g"]), "ln3b": f(inputs["ln3_b"]),
        "lnT": f(np.stack([np.asarray(inputs[k][0]).reshape(16, 128).T for k in
                           ("ln1_g", "ln1_b", "ln2_g", "ln2_b", "ln3_g", "ln3_b")], axis=1).reshape(128, 96)),
        "convw": f(np.asarray(inputs["conv_w"][0]).reshape(3, 8, 128).transpose(2, 1, 0).reshape(128, 24)),
        "gmix": f(np.asarray(inputs["mix_norm_g"][0]).reshape(16, 128).T),
        "sinks": f(inputs["attn_sinks"]),
        "ropec": c_t, "ropes": s_t,
    }
    maps = []
    xp_all = np.asarray(inputs["x_prompt"])
    xs_all = np.asarray(inputs["x_sample"])
    for c in range(ncores):
        m = dict(shared)
        m["xp"] = f(xp_all[2 * c:2 * c + 2].reshape(2 * SEQ, D))
        m["xs"] = f(xs_all[4 * c:4 * c + 4].reshape(256, D))
        m["cconv"] = f(inputs["cache_conv"][0][4 * c:4 * c + 4])
        m["ck"] = f(np.asarray(inputs["cache_k"][0][4 * c:4 * c + 4]).reshape(4, 128, 128))
        m["cv"] = f(np.asarray(inputs["cache_v"][0][4 * c:4 * c + 4]).reshape(4, 128, 128))
        maps.append(m)
    return maps


def kernel(**inputs):
    nc = build(full_groups())
    maps = make_in_maps(inputs)
    res = run_bass_kernel_spmd(nc, maps, core_ids=list(range(NCORES)))
    R = res.results
    y_p = np.concatenate([r["yp"].reshape(2, SEQ, D) for r in R], axis=0)
    y_s = np.concatenate([r["ys"].reshape(4, 64, D) for r in R], axis=0)
    ncp = np.concatenate([r["ncp"] for r in R], axis=0)[None]
    nkp = np.concatenate([r["nkp"].reshape(2, 128, 2, 64) for r in R], axis=0)[None]
    nvp = np.concatenate([r["nvp"].reshape(2, 128, 2, 64) for r in R], axis=0)[None]
    ncs = np.concatenate([r["ncs"] for r in R], axis=0)[None]
    nks = np.concatenate([r["nks"].reshape(4, 128, 2, 64) for r in R], axis=0)[None]
    nvs = np.concatenate([r["nvs"].reshape(4, 128, 2, 64) for r in R], axis=0)[None]
    return (y_p.astype(np.float32), y_s.astype(np.float32), ncp.astype(np.float32), nkp.astype(np.float32),
            nvp.astype(np.float32), ncs.astype(np.float32), nks.astype(np.float32), nvs.astype(np.float32))
```

```python
import numpy as np
import concourse.bass as bass
import concourse.mybir as mybir
from concourse.bass_utils import run_bass_kernel_spmd

F32 = mybir.dt.float32
BF16 = mybir.dt.bfloat16
AF = mybir.ActivationFunctionType
ALU = mybir.AluOpType

NCORES = 8
D = 2048
DFF = 5632
NKC = 16
NFC = 44
SEQ = 2048
ALPHA = 2.0 ** 0.25
C_FFN = 0.5 / ALPHA
C_MIX = 1.0 / ALPHA
LN_EPS_S = 1e-5 / (ALPHA * ALPHA)
RMS_EPS = 1e-6
NPOS = 2112


class SemC:
    __slots__ = ("h", "count")

    def __init__(self, h):
        self.h = h
        self.count = 0


class Res:
    __slots__ = ("name", "w", "r")

    def __init__(self, name):
        self.name = name
        self.w = None
        self.r = {}


class Eng:
    def __init__(self, nc, name):
        self.name = name
        self.q = []
        self.sem = SemC(nc.alloc_semaphore("prog_" + name))
        self.waited = {}
        self.dsems = []
        self.dnext = 0


class Prog:
    def __init__(self, nc):
        self.nc = nc
        self.pe = Eng(nc, "pe")
        self.act = Eng(nc, "act")
        self.dve = Eng(nc, "dve")
        self.pool = Eng(nc, "pool")
        self.sp = Eng(nc, "sp")
        self.sp.dsems = [SemC(nc.alloc_semaphore(f"dsp{i}")) for i in range(16)]
        self.pool.dsems = [SemC(nc.alloc_semaphore(f"dpl{i}")) for i in range(8)]

    def _waits(self, eng, reads, writes):
        deps = {}
        for r in reads:
            if r.w is not None:
                s, v = r.w
                if deps.get(s, 0) < v:
                    deps[s] = v
        for w in writes:
            if w.w is not None:
                s, v = w.w
                if deps.get(s, 0) < v:
                    deps[s] = v
            for s, v in w.r.items():
                if deps.get(s, 0) < v:
                    deps[s] = v
        for s, v in deps.items():
            self._wait(eng, s, v)

    def _wait(self, eng, s, v):
        if eng.waited.get(s, 0) < v:
            eng.waited[s] = v
            eng.q.append(lambda e, h=s.h, v=v: e.wait_ge(h, v))

    def _mark(self, tok, reads, writes):
        s, v = tok
        for r in reads:
            if r.r.get(s, 0) < v:
                r.r[s] = v
        for w in writes:
            w.w = tok
            w.r = {}

    def op(self, eng, fn, reads=(), writes=()):
        self._waits(eng, reads, writes)
        eng.sem.count += 1
        tok = (eng.sem, eng.sem.count)
        eng.q.append(lambda e, fn=fn, h=eng.sem.h: fn(e).then_inc(h, 1))
        self._mark(tok, reads, writes)

    def group(self, fns, reads=(), writes=()):
        eng = self.pe
        self._waits(eng, reads, writes)
        for fn in fns[:-1]:
            eng.q.append(lambda e, fn=fn: fn(e))
        eng.sem.count += 1
        tok = (eng.sem, eng.sem.count)
        eng.q.append(lambda e, fn=fns[-1], h=eng.sem.h: fn(e).then_inc(h, 1))
        self._mark(tok, reads, writes)

    def dma(self, eng, out, in_, reads=(), writes=(), **kw):
        self._waits(eng, reads, writes)
        s = eng.dsems[eng.dnext % len(eng.dsems)]
        eng.dnext += 1
        if s.count:
            self._wait(eng, s, s.count)
        s.count += 16
        tok = (s, s.count)
        eng.q.append(lambda e, o=out, i=in_, h=s.h, kw=kw: e.dma_start(out=o, in_=i, **kw).then_inc(h, 16))
        self._mark(tok, reads, writes)

    def finish(self):
        for eng in (self.sp, self.pool):
            for s in eng.dsems:
                if s.count:
                    self._wait(self.sp, s, s.count)
        for eng in (self.pe, self.act, self.dve, self.pool):
            if eng.sem.count:
                self._wait(self.sp, eng.sem, eng.sem.count)
        nc = self.nc
        with nc.Block() as block:
            @block.sync
            def _(e):
                for f in self.sp.q:
                    f(e)

            @block.tensor
            def _(e):
                for f in self.pe.q:
                    f(e)

            @block.scalar
            def _(e):
                for f in self.act.q:
                    f(e)

            @block.vector
            def _(e):
                for f in self.dve.q:
                    f(e)

            @block.gpsimd
            def _(e):
                for f in self.pool.q:
                    f(e)


def MM(out, lhsT, rhs, start, stop):
    return lambda e: e.matmul(out, lhsT=lhsT, rhs=rhs, start=start, stop=stop)


def TR(out, in_, ident):
    return lambda e: e.transpose(out=out, in_=in_, identity=ident)


def ACTV(out, in_, func, **kw):
    return lambda e: e.activation(out=out, in_=in_, func=func, **kw)


def TT(out, in0, in1, op):
    return lambda e: e.tensor_tensor(out=out, in0=in0, in1=in1, op=op)


def STT(out, in0, scalar, in1, op0, op1):
    return lambda e: e.scalar_tensor_tensor(out=out, in0=in0, scalar=scalar, in1=in1, op0=op0, op1=op1)


def TS(out, in0, s1, s2, op0, op1=None):
    if op1 is None:
        return lambda e: e.tensor_scalar(out=out, in0=in0, scalar1=s1, scalar2=s2, op0=op0)
    return lambda e: e.tensor_scalar(out=out, in0=in0, scalar1=s1, scalar2=s2, op0=op0, op1=op1)


def CP(out, in_):
    return lambda e: e.tensor_copy(out=out, in_=in_)


def MS(ap, val):
    return lambda e: e.memset(ap, val)


def RCP(out, in_):
    return lambda e: e.reciprocal(out=out, in_=in_)


def BNS(out, in_):
    return lambda e: e.bn_stats(out=out, in_=in_)


def BNA(out, in_):
    return lambda e: e.bn_aggr(out=out, in_=in_)


class _Stop(Exception):
    pass


def build(groups, stop=None):
    def chk(name):
        if stop is not None and name == stop:
            raise _Stop()
    nc = bass.Bass("TRN2", target_bir_lowering=False)
    P = Prog(nc)
    pe, act, dve, pool, sp = P.pe, P.act, P.dve, P.pool, P.sp
    dq = pool

    def din(name, shape, dt=F32):
        return nc.dram_tensor(name, list(shape), dt, kind="ExternalInput").ap()

    def dout(name, shape):
        return nc.dram_tensor(name, list(shape), F32, kind="ExternalOutput").ap()

    xp = din("xp", [2 * SEQ, D])
    xs = din("xs", [256, D])
    cconv = din("cconv", [4, 2, 1024])
    ck = din("ck", [4, 128, 128])
    cv = din("cv", [4, 128, 128])
    W = {
        "w1i": din("w1i", [D, 2 * DFF]), "w1o": din("w1o", [DFF, D]),
        "wmi": din("wmi", [D, 4352]), "wmo": din("wmo", [D, D]),
        "w2i": din("w2i", [D, 2 * DFF]), "w2o": din("w2o", [DFF, D]),
    }
    lnp = {k: din(k, [1, D]) for k in ("ln1g", "ln1b", "ln2g", "ln2b", "ln3g", "ln3b")}
    lnT_d = din("lnT", [128, 96])
    convw_d = din("convw", [128, 24])
    gmix_d = din("gmix", [128, 16])
    sinks_d = din("sinks", [1, 16])
    ropec = din("ropec", [128, NPOS])
    ropes = din("ropes", [128, NPOS])

    yp = dout("yp", [2 * SEQ, D])
    ys = dout("ys", [256, D])
    ncp = dout("ncp", [2, 2, 1024])
    nkp = dout("nkp", [2, 128, 128])
    nvp = dout("nvp", [2, 128, 128])
    ncs = dout("ncs", [4, 2, 1024])
    nks = dout("nks", [4, 128, 128])
    nvs = dout("nvs", [4, 128, 128])

    NT_W = {"w1i": 22, "w1o": 12, "wmi": 11, "wmo": 4, "w2i": 22, "w2o": 12}
    SC = {k: nc.dram_tensor("sc_" + k, [n, 128, 8192], BF16).ap() for k, n in NT_W.items()}
    R_sc = {}

    cur = [16640]

    def sb(name, shape, dt):
        nbytes = int(np.prod(shape[1:])) * (4 if dt == F32 else 2)
        off = cur[0]
        cur[0] = (off + nbytes + 63) // 64 * 64
        assert cur[0] <= 229376, (name, cur[0])
        return nc.alloc_sbuf_tensor_at(name, list(shape), dt, offset=off)

    def sb_at(name, shape, dt, off):
        return nc.alloc_sbuf_tensor_at(name, list(shape), dt, offset=off)

    xres = sb("xres", [128, 4, D], F32)
    xT = sb("xT", [128, NKC, 512], BF16)
    wsl = [sb(f"wsl{i}", [128, 8192], BF16) for i in range(3)]
    offD = cur[0]
    cur[0] += 49152
    hT = sb_at("hT", [128, NFC, 512], BF16, offD)
    sgt = [sb_at(f"sg{i}", [128, 512], F32, offD + 45056 + 2048 * i) for i in range(2)]
    ycT = sb_at("ycT", [128, 8, 512], F32, offD)
    yaT = sb_at("yaT", [128, 8, 512], F32, offD + 16384)
    ycb = sb_at("ycb", [128, 8, 512], BF16, offD + 32768)
    yab = sb_at("yab", [128, 8, 512], BF16, offD + 40960)
    offE = cur[0]
    cur[0] += 16384
    gbc = sb_at("gbc", [128, D], F32, offE)
    bbc = sb_at("bbc", [128, D], F32, offE + 8192)
    PT = [sb_at(f"PT{i}", [128, 16, 256], BF16, offE + 8192 * i) for i in range(2)]
    qTb = sb("qTb", [128, 8, 512], BF16)
    qs = sb("qs", [128, 512], F32)
    qr = sb("qr", [128, 512], F32)
    t1 = sb("t1", [128, 512], F32)
    cosg = sb("cosg", [128, 512], F32)
    sing = sb("sing", [128, 512], F32)
    kdup = [sb(f"kdup{g}", [128, 640], BF16) for g in range(2)]
    vaug = [[[sb(f"va{b}_{g}_{p}", [128, 128], BF16) for p in range(2)] for g in range(2)] for b in range(5)]
    uT = sb("uT", [128, 528], F32)
    cgs = sb("cgs", [128, 512], F32)
    zc = sb("zc", [128, 512], F32)
    sqt = [qs, qr]
    rinv = t1
    rd = sb("rd", [128, 512], F32)
    rd2 = sb("rd2", [128, 512], F32)
    kout = cgs
    vout = zc
    cstage = sb("cstage", [128, 128], F32)
    ident = sb("ident", [128, 128], F32)
    ones_f = sb("ones_f", [128, 128], F32)
    stats = sb("stats", [128, 4, 4, 6], F32)
    mv = sb("mv", [128, 4, 2], F32)
    rs = sb("rs", [128, 4], F32)
    convw = sb("convw_s", [128, 24], F32)
    gmix = sb("gmix_s", [128, 16], F32)
    ucar = sb("ucar", [128, 8, 4, 2], F32)
    sinkL = [sb(f"sinkL{p}", [2, 128], BF16) for p in range(2)]
    esrow = sb("esrow", [2, 16], BF16)
    lnT = sb("lnT_s", [128, 6, 16], F32)
    nmr = sb("nmr", [128, 4], F32)
    xstage = sb("xstage", [128, D], F32)
    sk = sb("sk", [2, 16], F32)
    skb = sb("skb", [2, 16], BF16)
    sk2 = sb("sk2", [2, 16], F32)
    msk = sb("msk", [2, 1], F32)
    Rperm = cstage

    banks = [nc.alloc_psum_tensor(f"bank{i}", [128, 512], F32) for i in range(8)]
    R_bank = [Res(f"bank{i}") for i in range(8)]
    ring = [0]

    reserved = set()

    ring_n = [7]

    def balloc():
        while True:
            b = ring[0] % ring_n[0]
            ring[0] += 1
            if b not in reserved:
                return b

    SSB = 7

    R_x = [Res(f"x{t}") for t in range(4)]
    R_xT = [Res(f"xT{c}") for c in range(NKC)]
    R_h = [Res(f"h{j}") for j in range(NFC)]
    R_w = [Res(f"w{i}") for i in range(3)]
    R_sg = [Res("sg0"), Res("sg1")]
    R_E = [Res("E0"), Res("E1")]
    R_stats = [Res(f"st{t}") for t in range(4)]
    R_mv = Res("mv")
    R_rs = Res("rs")
    R_qs, R_qr, R_t1, R_cs = Res("qs"), Res("qr"), Res("t1"), Res("cs")
    R_qT = [Res(f"qT{i}") for i in range(8)]
    R_kd = [Res("kd0"), Res("kd1")]
    R_va = [Res(f"va{b}") for b in range(5)]
    R_u, R_cg, R_z = Res("u"), Res("cg"), Res("z")
    R_yc = [Res(f"yc{i}") for i in range(8)]
    R_ya = [Res(f"ya{i}") for i in range(8)]
    R_ycb = [Res(f"ycb{i}") for i in range(8)]
    R_yab = [Res(f"yab{i}") for i in range(8)]
    R_sq = [R_qs, R_qr]
    R_rinv, R_rd = R_t1, Res("rd")
    R_rd2 = Res("rd2")
    rdbuf = [(rd, R_rd), (rd2, R_rd2)]
    rdi = [0]
    R_kout, R_vout, R_cst = R_cg, R_z, Res("cst")
    R_const = Res("const")
    R_ucar = Res("ucar")
    R_stage = Res("stage")
    R_nmr = Res("nmr")

    P.op(pool, MS(ident[:], 0.0), writes=[R_const])
    P.op(pool, lambda e: e.affine_select(out=ident[:], in_=ident[:], pattern=[[-1, 128]],
                                         compare_op=ALU.not_equal, fill=1.0, base=0,
                                         channel_multiplier=1), reads=[R_const], writes=[R_const])
    P.op(dve, MS(ones_f[:], 1.0), writes=[R_const])
    for b in range(5):
        for g in range(2):
            for p in range(2):
                P.op(pool, MS(vaug[b][g][p][:], 1.0), writes=[R_va[b]])
    P.op(dve, MS(sinkL[0][:], 1.0), writes=[R_const])
    P.op(dve, MS(sinkL[0][:, 0:64], 0.0), writes=[R_const])
    P.op(dve, MS(sinkL[1][:], 1.0), writes=[R_const])
    P.op(dve, MS(sinkL[1][:, 64:128], 0.0), writes=[R_const])
    P.op(dve, MS(msk[:], 0.0), writes=[R_const])
    P.op(dve, MS(msk[0:1, :], 1.0), writes=[R_const])
    P.dma(sp, convw[:], convw_d, writes=[R_const])
    P.dma(sp, gmix[:], gmix_d, writes=[R_const])
    P.dma(sp, sk[:], sinks_d.to_broadcast([2, 16]), writes=[R_const])
    P.op(act, ACTV(sk[:], sk[:], AF.Exp), reads=[R_const], writes=[R_const])
    P.op(dve, CP(skb[:], sk[:]), reads=[R_const], writes=[R_const])
    P.op(dve, CP(sk2[:], skb[:]), reads=[R_const], writes=[R_const])
    P.op(dve, TT(sk[:], sk[:], sk2[:], ALU.subtract), reads=[R_const], writes=[R_const])
    P.op(dve, TT(sk2[:], sk2[:], sk[:], ALU.subtract), reads=[R_const], writes=[R_const])
    P.op(dve, STT(sk[:], sk2[:], msk[:, 0:1], sk[:], ALU.mult, ALU.add), reads=[R_const], writes=[R_const])
    P.op(dve, CP(esrow[:], sk[:]), reads=[R_const], writes=[R_const])
    P.dma(sp, lnT[:].rearrange("p a b -> p (a b)"), lnT_d, writes=[R_const])

    def wspec(kind, idx):
        Wd = W[kind]
        kp = "(kc p) n -> p kc n"
        if kind in ("w1i", "w2i"):
            c0 = 256 * idx
            return [(0, (16, 256), Wd[:, c0:c0 + 256].rearrange(kp, p=128)),
                    (4096, (16, 256), Wd[:, DFF + c0:DFF + c0 + 256].rearrange(kp, p=128))]
        if kind in ("w1o", "w2o"):
            nb, jb = divmod(idx, 3)
            j0 = 16 * jb
            nj = 16 if jb < 2 else 12
            return [(0, (nj, 512), Wd[j0 * 128:(j0 + nj) * 128, nb * 512:(nb + 1) * 512]
                     .rearrange("(j p) n -> p j n", p=128))]
        if kind == "wmi":
            if idx == 0:
                return [(0, (16, 256), Wd[:, 4096:4352].rearrange(kp, p=128))]
            if idx in (1, 2):
                c0 = 3072 + 512 * (idx - 1)
                return [(0, (16, 512), Wd[:, c0:c0 + 512].rearrange(kp, p=128))]
            i = idx - 3
            return [(0, (16, 128), Wd[:, 1024 + 128 * i:1024 + 128 * i + 128].rearrange(kp, p=128)),
                    (2048, (16, 128), Wd[:, 2048 + 128 * i:2048 + 128 * i + 128].rearrange(kp, p=128)),
                    (4096, (16, 128), Wd[:, 128 * i:128 * i + 128].rearrange(kp, p=128))]
        if kind == "wmo":
            return [(0, (16, 512), Wd[:, idx * 512:(idx + 1) * 512].rearrange(kp, p=128))]
        raise ValueError(kind)

    wlist = []
    for _g in groups:
        for kind in ("w1i", "w1o", "wmi", "wmo", "w2i", "w2o"):
            for idx in range(NT_W[kind]):
                wlist.append((kind, idx))
    wstate = {"next": 0, "issued": 0}

    def wissue(n):
        kind, idx = wlist[n]
        slot = n % 3
        spec = wspec(kind, idx)
        ntot = max(off + a * b for off, (a, b), _ in spec)
        key = (kind, idx)
        if key not in R_sc:
            R_sc[key] = Res(f"sc_{kind}_{idx}")
            for off, (a, b), src in spec:
                dst = wsl[slot][:, off:off + a * b].rearrange("p (a b) -> p a b", b=b)
                P.dma(pool, dst, src, writes=[R_w[slot]])
            P.dma(sp, SC[kind][idx][:, 0:ntot], wsl[slot][:, 0:ntot], reads=[R_w[slot]], writes=[R_sc[key]])
        else:
            P.dma(sp, wsl[slot][:, 0:ntot], SC[kind][idx][:, 0:ntot], reads=[R_sc[key]], writes=[R_w[slot]])

    def wget(kind, idx):
        n = wstate["next"]
        assert wlist[n] == (kind, idx), (wlist[n], kind, idx)
        wstate["next"] += 1
        while wstate["issued"] < min(len(wlist), n + 3):
            wissue(wstate["issued"])
            wstate["issued"] += 1
        return n % 3

    def wview(slot, off, a, b):
        return wsl[slot][:, off:off + a * b].rearrange("p (a b) -> p a b", b=b)

    evac_rr = [0]

    def evac_copy(out, in_, reads, writes, eng=None):
        evac_rr[0] += 1
        use_act = (evac_rr[0] % 2 == 1) if eng is None else (eng == "act")
        if use_act:
            P.op(act, ACTV(out, in_, AF.Copy), reads=reads, writes=writes)
        else:
            P.op(dve, CP(out, in_), reads=reads, writes=writes)

    def to_xT(NT):
        for t in range(NT):
            xT_tile(t, lambda c, t=t: xres[:, t, c * 128:(c + 1) * 128], [R_x[t]], None)

    def xT_tile(t, src, src_res, ln_idx):
        for q in range(4):
            b = balloc()
            fns = [TR(banks[b][:, cc * 128:(cc + 1) * 128], src(4 * q + cc), ident[:]) for cc in range(4)]
            P.group(fns, reads=list(src_res) + [R_const], writes=[R_bank[b]])
            if ln_idx is None:
                evac_copy(xT[:, 4 * q:4 * q + 4, t * 128:(t + 1) * 128],
                          banks[b][:, :].rearrange("p (a c) -> p a c", c=128), [R_bank[b]],
                          [R_xT[4 * q + cc] for cc in range(4)], eng=("act" if q % 2 == 0 else "dve"))
            else:
                for cc in range(4):
                    c = 4 * q + cc
                    gs = lnT[:, 2 * ln_idx, c:c + 1]
                    bs = lnT[:, 2 * ln_idx + 1, c:c + 1]
                    if q % 2 == 0:
                        P.op(act, ACTV(xT[:, c, t * 128:(t + 1) * 128], banks[b][:, cc * 128:(cc + 1) * 128],
                                       AF.Identity, scale=gs, bias=bs), reads=[R_bank[b], R_const], writes=[R_xT[c]])
                    else:
                        P.op(dve, TS(xT[:, c, t * 128:(t + 1) * 128], banks[b][:, cc * 128:(cc + 1) * 128],
                                     gs, bs, ALU.mult, ALU.add), reads=[R_bank[b], R_const], writes=[R_xT[c]])

    def load_ln(gname, bname):
        P.dma(dq, gbc[:], lnp[gname].to_broadcast([128, D]), writes=[R_E[0]])
        P.dma(dq, bbc[:], lnp[bname].to_broadcast([128, D]), writes=[R_E[1]])

    def ffn_in(kind, NT):
        ntok = NT * 128
        for p in range(22):
            slot = wget(kind, p)
            wg = wview(slot, 0, 16, 256)
            wu = wview(slot, 4096, 16, 256)
            for jj in range(2):
                j = 2 * p + jj
                bg, bu = balloc(), balloc()
                fns = []
                for kc in range(NKC):
                    fns.append(MM(banks[bg][:, 0:ntok], wg[:, kc, jj * 128:(jj + 1) * 128], xT[:, kc, 0:ntok],
                                  kc == 0, kc == NKC - 1))
                for kc in range(NKC):
                    fns.append(MM(banks[bu][:, 0:ntok], wu[:, kc, jj * 128:(jj + 1) * 128], xT[:, kc, 0:ntok],
                                  kc == 0, kc == NKC - 1))
                P.group(fns, reads=[R_w[slot]] + R_xT, writes=[R_bank[bg], R_bank[bu]])
                k = j % 2
                P.op(act, ACTV(sgt[k][:, 0:ntok], banks[bg][:, 0:ntok], AF.Silu), reads=[R_bank[bg]], writes=[R_sg[k]])
                P.op(dve, TT(hT[:, j, 0:ntok], sgt[k][:, 0:ntok], banks[bu][:, 0:ntok], ALU.mult),
                     reads=[R_sg[k], R_bank[bu]], writes=[R_h[j]])

    def out_ln(kind, NT, nk, stat, stat_res, cres, ln_idx, final, ydst, nxt):
        tiles_per_nb = 3 if kind in ("w1o", "w2o") else 1
        for nb in range(4):
            if nxt is not None and nb < nxt["NT"]:
                P.dma(sp, xstage[:], nxt["xsrc"][nb * 128:(nb + 1) * 128, :], writes=[R_stage])
            bk = [balloc() for _ in range(NT)]
            k0 = 0
            for jb in range(tiles_per_nb):
                slot = wget(kind, nb * tiles_per_nb + jb)
                nj = (16 if jb < 2 else 12) if tiles_per_nb == 3 else 16
                wv = wview(slot, 0, nj, 512)
                halves = ((0, nj),) if kind != "wmo" else ((0, 8), (8, 16))
                for (ka, kb_) in halves:
                    for t in range(NT):
                        fns = [MM(banks[bk[t]][:, :], stat(k0 + kk, t), wv[:, kk, :], (k0 + kk == 0), (k0 + kk == nk - 1))
                               for kk in range(ka, kb_)]
                        P.group(fns, reads=[R_w[slot]] + stat_res[k0 + ka:k0 + kb_], writes=[R_bank[bk[t]]])
                k0 += nj
            sl = slice(nb * 512, (nb + 1) * 512)
            for t in range(NT):
                P.op(dve, STT(xres[:, t, sl], banks[bk[t]][:, :], cres, xres[:, t, sl], ALU.mult, ALU.add),
                     reads=[R_bank[bk[t]], R_x[t]], writes=[R_x[t]])
                P.op(dve, BNS(stats[:, t, nb, :], xres[:, t, sl]), reads=[R_x[t]], writes=[R_stats[t]])
            if nxt is not None and nb < nxt["NT"]:
                prefetch_xT(nxt, nb)
        for t in range(NT):
            P.op(dve, BNA(mv[:, t, :], stats[:, t, :, :].rearrange("p a b -> p (a b)")), reads=[R_stats[t]], writes=[R_mv])
        P.op(dve, TS(rs[:, 0:NT], mv[:, 0:NT, 1], LN_EPS_S, None, ALU.add), reads=[R_mv], writes=[R_rs])
        P.op(act, ACTV(rs[:, 0:NT], rs[:, 0:NT], AF.Ln), reads=[R_rs], writes=[R_rs])
        P.op(act, ACTV(rs[:, 0:NT], rs[:, 0:NT], AF.Exp, scale=-0.5), reads=[R_rs], writes=[R_rs])
        P.op(dve, STT(nmr[:, 0:NT], mv[:, 0:NT, 0], -1.0, rs[:, 0:NT], ALU.mult, ALU.mult),
             reads=[R_mv, R_rs], writes=[R_nmr])
        chk("ln_stats")
        for t in range(NT):
            if t % 2 == 0:
                P.op(act, ACTV(xres[:, t, :], xres[:, t, :], AF.Identity, scale=rs[:, t:t + 1], bias=nmr[:, t:t + 1]),
                     reads=[R_x[t], R_rs, R_nmr], writes=[R_x[t]])
            else:
                P.op(dve, TS(xres[:, t, :], xres[:, t, :], rs[:, t:t + 1], nmr[:, t:t + 1], ALU.mult, ALU.add),
                     reads=[R_x[t], R_rs, R_nmr], writes=[R_x[t]])
        chk("ln_norm")
        if not final:
            for t in range(NT):
                xT_tile(t, lambda c, t=t: xres[:, t, c * 128:(c + 1) * 128], [R_x[t]], ln_idx)
        chk("ln_xT")
        for t in range(NT):
            P.op(pool, TT(xres[:, t, :], xres[:, t, :], gbc[:], ALU.mult), reads=[R_x[t], R_E[0]], writes=[R_x[t]])
            P.op(pool, TT(xres[:, t, :], xres[:, t, :], bbc[:], ALU.add), reads=[R_x[t], R_E[1]], writes=[R_x[t]])
        if final:
            for t in range(NT):
                P.dma(dq, ydst[t * 128:(t + 1) * 128, :], xres[:, t, :], reads=[R_x[t]])

    def prefetch_xT(nxt, t):
        xT_tile(t, lambda c: xstage[:, c * 128:(c + 1) * 128], [R_stage], None)

    def rope_a(b, ntok):
        P.op(act, ACTV(qs[:, 0:ntok], banks[b][:, 0:ntok], AF.Copy), reads=[R_bank[b]], writes=[R_qs])

    def rope_b(ntok, out_ap, out_res):
        b2 = balloc()
        P.group([MM(banks[b2][:, 0:ntok], Rperm[:], qs[:, 0:ntok], True, True)], reads=[R_qs, R_cst],
                writes=[R_bank[b2]])
        P.op(dve, TT(t1[:, 0:ntok], qs[:, 0:ntok], cosg[:, 0:ntok], ALU.mult), reads=[R_qs, R_cs], writes=[R_t1])
        P.op(dve, TT(qr[:, 0:ntok], sing[:, 0:ntok], banks[b2][:, 0:ntok], ALU.mult), reads=[R_bank[b2], R_cs],
             writes=[R_qr])
        P.op(dve, TT(out_ap, t1[:, 0:ntok], qr[:, 0:ntok], ALU.add), reads=[R_t1, R_qr], writes=[out_res])

    def rms_scale(NT, yT, R_y, yb, R_yb, goff):
        ntok = NT * 128
        for i in range(8):
            k = i % 2
            P.op(act, ACTV(sqt[k][:, 0:ntok], yT[:, i, 0:ntok], AF.Square), reads=[R_y[i]], writes=[R_sq[k]])
            P.group([MM(banks[SSB][:, 0:ntok], ones_f[:], sqt[k][:, 0:ntok], i == 0, i == 7)],
                    reads=[R_sq[k], R_const], writes=[R_bank[SSB]])
        P.op(dve, TS(rinv[:, 0:ntok], banks[SSB][:, 0:ntok], 1.0 / 1024.0, RMS_EPS, ALU.mult, ALU.add),
             reads=[R_bank[SSB]], writes=[R_rinv])
        P.op(act, ACTV(rinv[:, 0:ntok], rinv[:, 0:ntok], AF.Ln), reads=[R_rinv], writes=[R_rinv])
        P.op(act, ACTV(rinv[:, 0:ntok], rinv[:, 0:ntok], AF.Exp, scale=-0.5), reads=[R_rinv], writes=[R_rinv])
        for i in range(8):
            P.op(dve, STT(yb[:, i, 0:ntok], yT[:, i, 0:ntok], gmix[:, goff + i:goff + i + 1], rinv[:, 0:ntok],
                          ALU.mult, ALU.mult), reads=[R_y[i], R_rinv, R_const], writes=[R_yb[i]])

    pend_norm = []

    def flush_norms():
        items = list(pend_norm)
        del pend_norm[:]
        for i0 in range(0, len(items), 2):
            pair = items[i0:i0 + 2]
            st = [pv_norm_a(*a_) for a_ in pair]
            for x in st:
                pv_norm_b(x)
            for x in st:
                pv_norm_c(x)
            for a_ in pair:
                reserved.discard(a_[0])

    def defer_norm(*a_):
        pend_norm.append(a_)
        reserved.add(a_[0])

    def pv_norm_a(b, par, g, ncols, qsl, nh_cols):
        orow = slice(0, 64) if par == 0 else slice(64, 128)
        drow = slice(64, 128) if par == 0 else slice(0, 64)
        rb, R_rb = rdbuf[rdi[0] % 2]
        rdi[0] += 1
        P.op(dve, CP(rb[orow, 0:ncols], banks[b][drow, 0:ncols]), reads=[R_bank[b]], writes=[R_rb])
        return (b, g, ncols, qsl, nh_cols, orow, rb, R_rb)

    def pv_norm_b(x):
        b, g, ncols, qsl, nh_cols, orow, rb, R_rb = x
        P.op(act, ACTV(rb[orow, 0:ncols], rb[orow, 0:ncols], AF.Ln), reads=[R_rb], writes=[R_rb])
        P.op(act, ACTV(rb[orow, 0:ncols], rb[orow, 0:ncols], AF.Exp, scale=-1.0), reads=[R_rb], writes=[R_rb])

    def pv_norm_c(x):
        b, g, ncols, qsl, nh_cols, orow, rb, R_rb = x
        P.op(dve, TT(yaT[orow, 4 * g:4 * g + 4, qsl],
                     banks[b][orow, 0:ncols].rearrange("p (a b) -> p a b", b=nh_cols),
                     rb[orow, 0:ncols].rearrange("p (a b) -> p a b", b=nh_cols), ALU.mult),
             reads=[R_bank[b], R_rb], writes=[R_ya[4 * g + jj] for jj in range(4)])

    def load_rope(G):
        for (d0, s0) in ((0, 32), (32, 0), (64, 96), (96, 64)):
            P.op(pool, CP(Rperm[:, d0:d0 + 32], ident[:, s0:s0 + 32]), reads=[R_const], writes=[R_cst])
        if G["kind"] == "p":
            p0 = G["gi"] * 512
            P.dma(dq, cosg[:, 0:512], ropec[:, p0:p0 + 512], writes=[R_cs])
            P.dma(dq, sing[:, 0:512], ropes[:, p0:p0 + 512], writes=[R_cs])
        else:
            for s_ in range(4):
                P.dma(dq, cosg[:, s_ * 64:(s_ + 1) * 64], ropec[:, 2048:2112], writes=[R_cs])
                P.dma(dq, sing[:, s_ * 64:(s_ + 1) * 64], ropes[:, 2048:2112], writes=[R_cs])

    def mix(G):
        kind = G["kind"]
        NT = 4 if kind == "p" else 2
        ntok = NT * 128
        NSEQ = 1 if kind == "p" else 4
        L = 512 if kind == "p" else 64
        gi = G["gi"]
        slot = wget("wmi", 0)
        wkv = wview(slot, 0, 16, 256)
        b = balloc()
        P.group([MM(banks[b][:, 0:ntok], wkv[:, kc, 0:128], xT[:, kc, 0:ntok], kc == 0, kc == NKC - 1)
                 for kc in range(NKC)], reads=[R_w[slot]] + R_xT, writes=[R_bank[b]])
        rope_a(b, ntok)
        rope_b(ntok, rd[:, 0:ntok], R_rd)
        for g in range(2):
            for half in range(2):
                eng = dve if half == 0 else pool
                P.op(eng, CP(kdup[g][half * 64:(half + 1) * 64, 128:128 + ntok], rd[g * 64:(g + 1) * 64, 0:ntok]),
                     reads=[R_rd], writes=[R_kd[g]])
        chk("kv_kd")
        if kind == "p" and G["last"]:
            bt = balloc()
            P.group([TR(banks[bt][:, 0:128], rd[:, 384:512], ident[:])], reads=[R_rd, R_const], writes=[R_bank[bt]])
            P.op(act, ACTV(kout[:, 0:128], banks[bt][:, 0:128], AF.Copy), reads=[R_bank[bt]], writes=[R_kout])
            P.dma(dq, nkp[G["seq"]], kout[:, 0:128], reads=[R_kout])
        if kind == "s":
            bt = balloc()
            P.group([TR(banks[bt][0:64, s * 128:(s + 1) * 128], rd[:, s * 64:(s + 1) * 64], ident[:]) for s in range(4)],
                    reads=[R_rd, R_const], writes=[R_bank[bt]])
            P.op(act, ACTV(kout[0:64, :], banks[bt][0:64, :], AF.Copy), reads=[R_bank[bt]], writes=[R_kout])
            P.dma(dq, nks[:, 64:128, :].rearrange("s t f -> t s f"), kout[0:64, :].rearrange("p (s f) -> p s f", f=128),
                  reads=[R_kout])
            chk("kv_kout")
            stg = uT[0:64, 0:512].rearrange("p (s f) -> p s f", f=128)
            for src_, dst_ in ((ck, nks), (cv, nvs)):
                P.dma(dq, stg, src_[:, 64:128, :].rearrange("s t f -> t s f"), writes=[R_u])
                P.dma(dq, dst_[:, 0:64, :].rearrange("s t f -> t s f"), stg, reads=[R_u])
        chk("kv_roll")
        bv = balloc()
        if kind == "p":
            fns = []
            for t in range(NT):
                for kc in range(NKC):
                    fns.append(MM(banks[bv][:, t * 128:(t + 1) * 128], xT[:, kc, t * 128:(t + 1) * 128],
                                  wkv[:, kc, 128:256], kc == 0, kc == NKC - 1))
            P.group(fns, reads=[R_w[slot]] + R_xT, writes=[R_bank[bv]])
            for t in range(NT):
                for g in range(2):
                    for p in range(2):
                        csl = slice(0, 64) if p == 0 else slice(64, 128)
                        evac_copy(vaug[t + 1][g][p][:, csl], banks[bv][:, t * 128 + g * 64:t * 128 + g * 64 + 64],
                                  [R_bank[bv]], [R_va[t + 1]], eng="act")
            if G["last"]:
                P.op(act, ACTV(vout[:, 0:128], banks[bv][:, 384:512], AF.Copy), reads=[R_bank[bv]], writes=[R_vout])
                P.dma(dq, nvp[G["seq"]], vout[:, 0:128], reads=[R_vout])
        else:
            fns = []
            for s in range(4):
                for kc in range(NKC):
                    fns.append(MM(banks[bv][0:64, s * 128:(s + 1) * 128], xT[:, kc, s * 64:(s + 1) * 64],
                                  wkv[:, kc, 128:256], kc == 0, kc == NKC - 1))
            P.group(fns, reads=[R_w[slot]] + R_xT, writes=[R_bank[bv]])
            chk("kv_vmm")
            for s in range(4):
                for g in range(2):
                    for p in range(2):
                        csl = slice(0, 64) if p == 0 else slice(64, 128)
                        evac_copy(vaug[s][g][p][0:64, csl], banks[bv][0:64, s * 128 + g * 64:s * 128 + g * 64 + 64],
                                  [R_bank[bv]], [R_va[s]], eng="dve")
            chk("kv_vaug")
            P.op(dve, CP(vout[0:64, :], banks[bv][0:64, :]), reads=[R_bank[bv]], writes=[R_vout])
            chk("kv_vact")
            P.dma(dq, nvs[:, 64:128, :].rearrange("s t f -> t s f"), vout[0:64, :].rearrange("p (s f) -> p s f", f=128),
                  reads=[R_vout])

        chk("mix_kv")
        for qt in range(2):
            slot = wget("wmi", 1 + qt)
            wq = wview(slot, 0, 16, 512)
            for ii in range(4):
                i = 4 * qt + ii
                b = balloc()
                P.group([MM(banks[b][:, 0:ntok], wq[:, kc, ii * 128:(ii + 1) * 128], xT[:, kc, 0:ntok],
                            kc == 0, kc == NKC - 1) for kc in range(NKC)],
                        reads=[R_w[slot]] + R_xT, writes=[R_bank[b]])
                if i > 0:
                    rope_b(ntok, qTb[:, i - 1, 0:ntok], R_qT[i - 1])
                rope_a(b, ntok)
        rope_b(ntok, qTb[:, 7, 0:ntok], R_qT[7])

        chk("mix_q")
        uv = uT[:, 0:NSEQ * (L + 2)].rearrange("p (s l) -> p s l", l=L + 2)
        zv = zc[:, 0:ntok].rearrange("p (s l) -> p s l", l=L)
        if kind == "s":
            for s in range(4):
                for r in range(2):
                    P.dma(dq, ucar[:, :, s, r], cconv[s, r].rearrange("(i p) -> p i", p=128), writes=[R_ucar],
                          allow_slow_non_contiguous=True)
        agen = attn_prompt(G) if kind == "p" else attn_sample(G)
        ring_n[0] = 8
        for i in range(8):
            if i > 0:
                next(agen, None)
            if next(agen, "done") == "done":
                flush_norms()
            slot = wget("wmi", 3 + i)
            wc = wview(slot, 0, 16, 128)
            wh = wview(slot, 2048, 16, 128)
            wb = wview(slot, 4096, 16, 128)
            b1, b2, b3 = balloc(), balloc(), balloc()
            fns = []
            for (wv_, bb) in ((wc, b1), (wh, b2), (wb, b3)):
                for kc in range(NKC):
                    fns.append(MM(banks[bb][:, 0:ntok], wv_[:, kc, :], xT[:, kc, 0:ntok], kc == 0, kc == NKC - 1))
            P.group(fns, reads=[R_w[slot]] + R_xT, writes=[R_bank[b1], R_bank[b2], R_bank[b3]])
            P.op(act, ACTV(cgs[:, 0:ntok], banks[b1][:, 0:ntok], AF.Copy), reads=[R_bank[b1]], writes=[R_cg])
            if kind == "p" and gi == 0:
                P.op(pool, MS(uT[:, 0:2], 0.0), writes=[R_u])
            else:
                P.op(pool, CP(uv[:, :, 0:2], ucar[:, i, 0:NSEQ, :]), reads=[R_ucar], writes=[R_u])
            P.op(dve, TT(uv[:, :, 2:L + 2], cgs[:, 0:ntok].rearrange("p (s l) -> p s l", l=L),
                         banks[b2][:, 0:ntok].rearrange("p (s l) -> p s l", l=L), ALU.mult),
                 reads=[R_cg, R_bank[b2]], writes=[R_u])
            P.op(dve, TS(zv, uv[:, :, 0:L], convw[:, 3 * i:3 * i + 1], None, ALU.mult),
                 reads=[R_u, R_const], writes=[R_z])
            P.op(dve, STT(zv, uv[:, :, 1:L + 1], convw[:, 3 * i + 1:3 * i + 2], zv, ALU.mult, ALU.add),
                 reads=[R_u, R_z, R_const], writes=[R_z])
            P.op(dve, STT(zv, uv[:, :, 2:L + 2], convw[:, 3 * i + 2:3 * i + 3], zv, ALU.mult, ALU.add),
                 reads=[R_u, R_z, R_const], writes=[R_z])
            P.op(pool, CP(ucar[:, i, 0:NSEQ, :], uv[:, :, L:L + 2]), reads=[R_u], writes=[R_ucar])
            P.op(dve, TT(ycT[:, i, 0:ntok], zc[:, 0:ntok], banks[b3][:, 0:ntok], ALU.mult),
                 reads=[R_z, R_bank[b3]], writes=[R_yc[i]])
        ring_n[0] = 7
        rms_scale(NT, ycT, R_yc, ycb, R_ycb, 0)
        for _ in agen:
            pass
        flush_norms()
        if kind == "p" and G["last"]:
            for r in range(2):
                P.dma(dq, ncp[G["seq"], r].rearrange("(i p) -> p i", p=128), ucar[:, :, 0, r], reads=[R_ucar],
                      allow_slow_non_contiguous=True)
        if kind == "s":
            for s in range(4):
                for r in range(2):
                    P.dma(dq, ncs[s, r].rearrange("(i p) -> p i", p=128), ucar[:, :, s, r], reads=[R_ucar],
                          allow_slow_non_contiguous=True)

        chk("mix_attn")
        rms_scale(NT, yaT, R_ya, yab, R_yab, 8)
        if kind == "p" and not G["last"]:
            for g in range(2):
                P.op(pool, CP(kdup[g][:, 0:128], kdup[g][:, 512:640]), reads=[R_kd[g]], writes=[R_kd[g]])
                for p in range(2):
                    P.op(pool, CP(vaug[0][g][p][:], vaug[4][g][p][:]), reads=[R_va[4]], writes=[R_va[0]])

    def attn_prompt(G):
        gi = G["gi"]
        kb0 = 0 if gi > 0 else 1
        info = {}
        for kbi in range(kb0, 5):
            kb = kbi - 1
            lo = max(0, 2 * kb)
            hi = min(8, 2 * kb + 4)
            ncols = (hi - lo) * 64
            col0 = lo * 64
            info[kbi] = lo
            pt = PT[kbi % 2]
            RE = R_E[kbi % 2]
            for ip in range(4):
                bA, bB = balloc(), balloc()
                g = ip // 2
                fns = []
                for hh in range(2):
                    hq = 2 * ip + hh
                    for e_, bb in ((0, bA), (1, bB)):
                        fns.append(MM(banks[bb][:, hh * 256:hh * 256 + ncols],
                                      kdup[g][e_ * 64:(e_ + 1) * 64, kbi * 128:(kbi + 1) * 128],
                                      qTb[e_ * 64:(e_ + 1) * 64, hq, col0:col0 + ncols], True, True))
                P.group(fns, reads=[R_kd[g], R_qT[2 * ip], R_qT[2 * ip + 1]], writes=[R_bank[bA], R_bank[bB]])
                for e_, bb in ((0, bA), (1, bB)):
                    P.op(act, ACTV(pt[:, 4 * ip + e_:4 * ip + e_ + 3:2, 0:ncols],
                                   banks[bb][:, :].rearrange("p (a c) -> p a c", c=256)[:, :, 0:ncols],
                                   AF.Exp, scale=0.125), reads=[R_bank[bb]], writes=[RE])
            chk("at_s")
            for half in range(2):
                kc_ = 2 * kb + half
                for qc in range(lo, hi):
                    if not (kc_ <= qc <= kc_ + 2):
                        c0 = (qc - lo) * 64
                        P.op(pool, MS(pt[half * 64:(half + 1) * 64, :, c0:c0 + 64], 0.0), writes=[RE])
            flush_norms()
            yield
            m = kbi - 1
            if m < 0:
                yield
                continue
            qsl = slice(m * 128, (m + 1) * 128)
            contrib = []
            if (kbi - 1) in info:
                contrib.append((kbi - 1, (2 * m - info[kbi - 1]) * 64))
            contrib.append((kbi, (2 * m - lo) * 64))
            for g in range(2):
                for par in range(2):
                    b = balloc()
                    h0 = 8 * g + par
                    ob = banks[b][:, :].rearrange("p (a c) -> p a c", c=128)
                    fns = []
                    for ci, (kk, c0) in enumerate(contrib):
                        fns.append(MM(ob, vaug[kk][g][par][:], PT[kk % 2][:, h0:h0 + 7:2, c0:c0 + 128], ci == 0, False))
                    fns.append(MM(ob, sinkL[par][:], esrow[:, h0:h0 + 7:2].unsqueeze(2).to_broadcast([2, 4, 128]), False, True))
                    P.group(fns, reads=[R_E[kk % 2] for kk, _ in contrib] + [R_va[kk] for kk, _ in contrib] + [R_const],
                            writes=[R_bank[b]])
                    defer_norm(b, par, g, 512, qsl, 128)
            yield

    def attn_sample(G):
        for s in range(4):
            P.dma(dq, cstage[:], ck[s], writes=[R_cst])
            bt = balloc()
            P.group([TR(banks[bt][:, 0:128], cstage[:], ident[:])], reads=[R_cst, R_const], writes=[R_bank[bt]])
            for g in range(2):
                for half in range(2):
                    P.op(dve, CP(kdup[g][half * 64:(half + 1) * 64, 384:512], banks[bt][g * 64:(g + 1) * 64, 0:128]),
                         reads=[R_bank[bt]], writes=[R_kd[g]])
            P.dma(dq, cstage[:], cv[s], writes=[R_cst])
            for g in range(2):
                for p in range(2):
                    csl = slice(0, 64) if p == 0 else slice(64, 128)
                    evac_copy(vaug[4][g][p][:, csl], cstage[:, g * 64:(g + 1) * 64], [R_cst], [R_va[4]])
            for blk in range(2):
                nk_ = 128 if blk == 0 else 64
                ksl = slice(384, 512) if blk == 0 else slice(128 + s * 64, 128 + (s + 1) * 64)
                bA, bB = balloc(), balloc()
                fns = []
                for hh in range(8):
                    for e_, bb in ((0, bA), (1, bB)):
                        h = 2 * hh + e_
                        g = h // 8
                        fns.append(MM(banks[bb][0:nk_, hh * 64:(hh + 1) * 64], kdup[g][e_ * 64:(e_ + 1) * 64, ksl],
                                      qTb[e_ * 64:(e_ + 1) * 64, hh, s * 64:(s + 1) * 64], True, True))
                P.group(fns, reads=R_kd + R_qT, writes=[R_bank[bA], R_bank[bB]])
                for e_, bb in ((0, bA), (1, bB)):
                    P.op(act, ACTV(PT[blk][0:nk_, e_:16:2, 0:64],
                                   banks[bb][0:nk_, :].rearrange("p (a c) -> p a c", c=64), AF.Exp, scale=0.125),
                         reads=[R_bank[bb]], writes=[R_E[blk]])
            flush_norms()
            yield
            qsl = slice(s * 64, (s + 1) * 64)
            for g in range(2):
                for par in range(2):
                    b = balloc()
                    h0 = 8 * g + par
                    ob = banks[b][:, 0:256].rearrange("p (a c) -> p a c", c=64)
                    fns = [MM(ob, vaug[4][g][par][:], PT[0][:, h0:h0 + 7:2, 0:64], True, False),
                           MM(ob, vaug[s][g][par][0:64, :], PT[1][0:64, h0:h0 + 7:2, 0:64], False, False),
                           MM(ob, sinkL[par][:], esrow[:, h0:h0 + 7:2].unsqueeze(2).to_broadcast([2, 4, 64]), False, True)]
                    P.group(fns, reads=[R_E[0], R_E[1], R_va[4], R_va[s], R_const], writes=[R_bank[b]])
                    defer_norm(b, par, g, 256, qsl, 64)
            yield

    RY = R_ycb + R_yab

    def mstat(k, t):
        return ycb[:, k, t * 128:(t + 1) * 128] if k < 8 else yab[:, k - 8, t * 128:(t + 1) * 128]

    def hstat(k, t):
        return hT[:, k, t * 128:(t + 1) * 128]

    for G in groups:
        G["NT"] = 4 if G["kind"] == "p" else 2
        if G["kind"] == "p":
            r0 = G["seq"] * SEQ + G["gi"] * 512
            G["xsrc"], G["ydst"] = xp[r0:r0 + 512, :], yp[r0:r0 + 512, :]
        else:
            G["xsrc"], G["ydst"] = xs, ys
    try:
        for gidx, G in enumerate(groups):
            NT = G["NT"]
            xsrc, ydst = G["xsrc"], G["ydst"]
            nxt = groups[gidx + 1] if gidx + 1 < len(groups) else None
            chk("setup")
            for t in range(NT):
                P.dma(dq, xres[:, t, :], xsrc[t * 128:(t + 1) * 128, :], writes=[R_x[t]])
            chk("xload")
            if gidx == 0:
                to_xT(NT)
            chk("xT")
            load_ln("ln1g", "ln1b")
            load_rope(G)
            ffn_in("w1i", NT)
            chk("ffn1in")
            out_ln("w1o", NT, NFC, hstat, R_h, C_FFN, 0, False, None, None)
            chk("ffn1out")
            mix(G)
            chk("mix")
            load_ln("ln2g", "ln2b")
            out_ln("wmo", NT, 16, mstat, RY, C_MIX, 1, False, None, None)
            chk("mixout")
            load_ln("ln3g", "ln3b")
            ffn_in("w2i", NT)
            chk("ffn2in")
            out_ln("w2o", NT, NFC, hstat, R_h, C_FFN, 2, True, ydst, nxt)
    except _Stop:
        pass
    P.finish()
    return nc


def full_groups():
    gs = []
    for s in range(2):
        for gi in range(4):
            gs.append({"kind": "p", "seq": s, "gi": gi, "last": gi == 3})
    gs.append({"kind": "s", "seq": 0, "gi": 0, "last": True})
    return gs


def rope_tables():
    half = 32
    inv = (np.float32(10000.0) ** (-(np.arange(half, dtype=np.float32) / np.float32(half)))).astype(np.float32)
    pos = np.arange(NPOS, dtype=np.float32)
    ang = (pos[None, :] * inv[:, None]).astype(np.float32)
    cos = np.cos(ang).astype(np.float32)
    sin = np.sin(ang).astype(np.float32)
    c = np.zeros((128, NPOS), np.float32)
    s = np.zeros((128, NPOS), np.float32)
    for p in range(128):
        d = p % 64
        c[p] = cos[d % 32]
        s[p] = -sin[d % 32] if d < 32 else sin[d % 32]
    return c, s


def make_in_maps(inputs, ncores=NCORES):
    f = lambda a: np.ascontiguousarray(np.asarray(a, dtype=np.float32))
    c_t, s_t = rope_tables()
    shared = {
        "w1i": f(inputs["ffn1_w_in"][0]), "w1o": f(inputs["ffn1_w_out"][0]),
        "wmi": f(inputs["w_mix_in"][0]), "wmo": f(inputs["w_mix_out"][0]),
        "w2i": f(inputs["ffn2_w_in"][0]), "w2o": f(inputs["ffn2_w_out"][0]),
        "ln1g": f(inputs["ln1_g"]), "ln1b": f(inputs["ln1_b"]),
        "ln2g": f(inputs["ln2_g"]), "ln2b": f(inputs["ln2_b"]),
        "ln3g": f(inputs["ln3_g"]), "ln3b": f(inputs["ln3_b"]),
        "lnT": f(np.stack([np.asarray(inputs[k][0]).reshape(16, 128).T for k in
                           ("ln1_g", "ln1_b", "ln2_g", "ln2_b", "ln3_g", "ln3_b")], axis=1).reshape(128, 96)),
        "convw": f(np.asarray(inputs["conv_w"][0]).reshape(3, 8, 128).transpose(2, 1, 0).reshape(128, 24)),
        "gmix": f(np.asarray(inputs["mix_norm_g"][0]).reshape(16, 128).T),
        "sinks": f(inputs["attn_sinks"]),
        "ropec": c_t, "ropes": s_t,
    }
    maps = []
    xp_all = np.asarray(inputs["x_prompt"])
    xs_all = np.asarray(inputs["x_sample"])
    for c in range(ncores):
        m = dict(shared)
        m["xp"] = f(xp_all[2 * c:2 * c + 2].reshape(2 * SEQ, D))
        m["xs"] = f(xs_all[4 * c:4 * c + 4].reshape(256, D))
        m["cconv"] = f(inputs["cache_conv"][0][4 * c:4 * c + 4])
        m["ck"] = f(np.asarray(inputs["cache_k"][0][4 * c:4 * c + 4]).reshape(4, 128, 128))
        m["cv"] = f(np.asarray(inputs["cache_v"][0][4 * c:4 * c + 4]).reshape(4, 128, 128))
        maps.append(m)
    return maps


def kernel(**inputs):
    nc = build(full_groups())
    maps = make_in_maps(inputs)
    res = run_bass_kernel_spmd(nc, maps, core_ids=list(range(NCORES)))
    R = res.results
    y_p = np.concatenate([r["yp"].reshape(2, SEQ, D) for r in R], axis=0)
    y_s = np.concatenate([r["ys"].reshape(4, 64, D) for r in R], axis=0)
    ncp = np.concatenate([r["ncp"] for r in R], axis=0)[None]
    nkp = np.concatenate([r["nkp"].reshape(2, 128, 2, 64) for r in R], axis=0)[None]
    nvp = np.concatenate([r["nvp"].reshape(2, 128, 2, 64) for r in R], axis=0)[None]
    ncs = np.concatenate([r["ncs"] for r in R], axis=0)[None]
    nks = np.concatenate([r["nks"].reshape(4, 128, 2, 64) for r in R], axis=0)[None]
    nvs = np.concatenate([r["nvs"].reshape(4, 128, 2, 64) for r in R], axis=0)[None]
    return (y_p.astype(np.float32), y_s.astype(np.float32), ncp.astype(np.float32), nkp.astype(np.float32),
            nvp.astype(np.float32), ncs.astype(np.float32), nks.astype(np.float32), nvs.astype(np.float32))
```
